# Optimizing a Trainium2 kernel written in Bass

```python
import jax
import jax.numpy as jnp
from jax import lax
import numpy as np

D_MODEL = 4096
BATCH = 1
SEQ = 8192
DEPTH = 2

N_MIXERS = 2
D_FF = 256 * (-(-(8 * D_MODEL) // (3 * 256)))
LN_EPS = 1e-5
RMS_EPS = 1e-6
ALPHA = float((2 * DEPTH) ** 0.25)
BETA = float((8 * DEPTH) ** -0.25)
HG_HEAD_DIM = 128
HG_HEADS = D_MODEL // HG_HEAD_DIM
HG_CHUNK = 64
MB_HEAD_DIM = 128
MB_HEADS = D_MODEL // MB_HEAD_DIM
MB_BLOCK = 256
MB_TOPK = 3
MB_QCHUNK = 16

kernel_name = 'hybrid_hgrn2_moba_macaron_deepnorm'


def layer_norm(x, g, b):
    xf = x.astype(jnp.float32)
    mu = jnp.mean(xf, axis=-1, keepdims=True)
    var = jnp.mean(jnp.square(xf - mu), axis=-1, keepdims=True)
    y = (xf - mu) * lax.rsqrt(var + LN_EPS)
    return (y * g.astype(jnp.float32) + b.astype(jnp.float32)).astype(x.dtype)


def swiglu_ffn(x, w_in, w_out):
    gate, up = jnp.split(x @ w_in, 2, axis=-1)
    return ((jax.nn.silu(gate) * up) @ w_out).astype(x.dtype)


def hgrn_lower_bound(lb_logits, layer_idx):
    p = jax.nn.softmax(lb_logits.astype(jnp.float32), axis=0)
    return jnp.cumsum(p, axis=0)[layer_idx]


def hgrn2_mixer(x, w_in, norm_g, w_out, lower_bound):
    B, S, D = x.shape
    H, dk, C = HG_HEADS, HG_HEAD_DIM, HG_CHUNK
    Sp = -(-S // C) * C
    n = Sp // C
    xp = jnp.pad(x, ((0, 0), (0, Sp - S), (0, 0)))
    q, f, v, g = jnp.split((xp @ w_in).astype(jnp.float32), 4, axis=-1)
    q = jax.nn.silu(q)
    forget = lower_bound + (1.0 - lower_bound) * jax.nn.sigmoid(f)
    k = 1.0 - forget
    log_f = jnp.log(forget)

    def chunks(t):
        return t.reshape(B, n, C, H, dk).transpose(1, 0, 3, 2, 4)

    causal = jnp.tril(jnp.ones((C, C), dtype=bool))[:, :, None]

    def step(state, inp):
        qc, kc, vc, lc = inp
        cum = jnp.cumsum(lc, axis=2)
        rel = cum[:, :, :, None, :] - cum[:, :, None, :, :]
        decay = jnp.exp(jnp.where(causal, rel, -jnp.inf))
        scores = jnp.einsum('bhtsd,bhsd->bhts', qc[:, :, :, None, :] * decay, kc)
        o = (jnp.einsum('bhts,bhse->bhte', scores, vc)
             + jnp.einsum('bhtd,bhde->bhte', qc * jnp.exp(cum), state))
        last = cum[:, :, -1:, :]
        state = (state * jnp.exp(last)[:, :, 0, :, None]
                 + jnp.einsum('bhsd,bhse->bhde', kc * jnp.exp(last - cum), vc))
        return state, o

    s0 = jnp.zeros((B, H, dk, dk), jnp.float32)
    _, o = lax.scan(step, s0, (chunks(q), chunks(k), chunks(v), chunks(log_f)))
    o = o.transpose(1, 0, 3, 2, 4).reshape(B, Sp, H, dk)
    o = o * lax.rsqrt(jnp.mean(jnp.square(o), axis=-1, keepdims=True) + RMS_EPS)
    o = o * norm_g.astype(jnp.float32).reshape(H, dk)
    o = o.reshape(B, Sp, D) * jax.nn.silu(g)
    return (o[:, :S] @ w_out.astype(jnp.float32)).astype(x.dtype)


def moba_mixer(x, w_in, w_out):
    B, S, D = x.shape
    H, dh, L, QC = MB_HEADS, MB_HEAD_DIM, MB_BLOCK, MB_QCHUNK
    Sp = -(-S // L) * L
    nb, nc = Sp // L, Sp // QC
    n_sel = min(MB_TOPK, nb)
    scale = dh ** -0.5
    xp = jnp.pad(x, ((0, 0), (0, Sp - S), (0, 0)))
    qkv = (xp @ w_in).astype(jnp.float32).reshape(B, Sp, 3, H, dh).transpose(2, 0, 3, 1, 4)
    q, k, v = qkv[0], qkv[1], qkv[2]
    kb = k.reshape(B, H, nb, L, dh)
    vb = v.reshape(B, H, nb, L, dh)

    k_mean = jnp.mean(kb, axis=3)
    gate = jnp.einsum('bhtd,bhnd->bhtn', q, k_mean)
    q_blk = jnp.arange(Sp) // L
    fully_past = jnp.arange(nb)[None, :] < q_blk[:, None]
    gate = jnp.where(fully_past, gate, -jnp.inf)
    _, sel = lax.top_k(gate, n_sel)

    q_c = q.reshape(B, H, nc, QC, dh).transpose(2, 0, 1, 3, 4)
    sel_c = sel.reshape(B, H, nc, QC, n_sel).transpose(2, 0, 1, 3, 4)
    b_idx = jnp.arange(B)[:, None, None, None]
    h_idx = jnp.arange(H)[None, :, None, None]

    def attend(args):
        c, qq, ss = args
        j = (c * QC) // L
        k_sel = kb[b_idx, h_idx, ss]
        v_sel = vb[b_idx, h_idx, ss]
        s_sel = jnp.einsum('bhqd,bhqnkd->bhqnk', qq, k_sel) * scale
        s_sel = jnp.where((ss < j)[..., None], s_sel, -jnp.inf)
        k_own = lax.dynamic_index_in_dim(kb, j, axis=2, keepdims=False)
        v_own = lax.dynamic_index_in_dim(vb, j, axis=2, keepdims=False)
        s_own = jnp.einsum('bhqd,bhkd->bhqk', qq, k_own) * scale
        q_pos = c * QC + jnp.arange(QC)
        k_pos = j * L + jnp.arange(L)
        s_own = jnp.where(k_pos[None, :] <= q_pos[:, None], s_own, -jnp.inf)
        s_all = jnp.concatenate([s_sel.reshape(B, H, QC, n_sel * L), s_own], axis=-1)
        p = jax.nn.softmax(s_all, axis=-1)
        p_sel = p[..., :n_sel * L].reshape(B, H, QC, n_sel, L)
        p_own = p[..., n_sel * L:]
        return (jnp.einsum('bhqnk,bhqnkd->bhqd', p_sel, v_sel)
                + jnp.einsum('bhqk,bhkd->bhqd', p_own, v_own))

    o = lax.map(attend, (jnp.arange(nc), q_c, sel_c))
    o = o.transpose(1, 0, 3, 2, 4).reshape(B, Sp, H * dh)[:, :S]
    return (o @ w_out.astype(jnp.float32)).astype(x.dtype)


def setup_inputs(seed: int = 0) -> dict:
    key = jax.random.key(seed)
    ks = iter(jax.random.split(key, 64))
    f32 = jnp.float32
    D, F = D_MODEL, D_FF

    def dense(shape, fan_in, scale=1.0):
        return jax.random.normal(next(ks), shape, f32) * (scale * fan_in ** -0.5)

    def gain(n):
        return 1.0 + 0.02 * jax.random.normal(next(ks), (n,), f32)

    def bias(n):
        return 0.02 * jax.random.normal(next(ks), (n,), f32)

    inp = {}
    inp['x'] = jax.random.normal(next(ks), (BATCH, SEQ, D), f32)
    inp['lb_logits'] = jax.random.normal(next(ks), (DEPTH + 1, D), f32)
    inp['l0_ffn1_in'] = dense((D, 2 * F), D)
    inp['l0_ffn1_out'] = dense((F, D), F, BETA)
    inp['l0_ln1_g'] = gain(D)
    inp['l0_ln1_b'] = bias(D)
    inp['l0_hg_in'] = dense((D, 4 * D), D)
    inp['l0_hg_norm_g'] = gain(D)
    inp['l0_hg_out'] = dense((D, D), D, BETA)
    inp['l0_ln2_g'] = gain(D)
    inp['l0_ln2_b'] = bias(D)
    inp['l0_ffn2_in'] = dense((D, 2 * F), D)
    inp['l0_ffn2_out'] = dense((F, D), F, BETA)
    inp['l0_ln3_g'] = gain(D)
    inp['l0_ln3_b'] = bias(D)
    inp['l1_ffn1_in'] = dense((D, 2 * F), D)
    inp['l1_ffn1_out'] = dense((F, D), F, BETA)
    inp['l1_ln1_g'] = gain(D)
    inp['l1_ln1_b'] = bias(D)
    inp['l1_mb_in'] = dense((D, 3 * D), D)
    inp['l1_mb_out'] = dense((D, D), D, BETA)
    inp['l1_ln2_g'] = gain(D)
    inp['l1_ln2_b'] = bias(D)
    inp['l1_ffn2_in'] = dense((D, 2 * F), D)
    inp['l1_ffn2_out'] = dense((F, D), F, BETA)
    inp['l1_ln3_g'] = gain(D)
    inp['l1_ln3_b'] = bias(D)
    return inp


def reference(x, lb_logits,
              l0_ffn1_in, l0_ffn1_out, l0_ln1_g, l0_ln1_b, l0_hg_in, l0_hg_norm_g, l0_hg_out,
              l0_ln2_g, l0_ln2_b, l0_ffn2_in, l0_ffn2_out, l0_ln3_g, l0_ln3_b,
              l1_ffn1_in, l1_ffn1_out, l1_ln1_g, l1_ln1_b, l1_mb_in, l1_mb_out,
              l1_ln2_g, l1_ln2_b, l1_ffn2_in, l1_ffn2_out, l1_ln3_g, l1_ln3_b):
    ffn_pre = ((l0_ffn1_in, l0_ffn1_out), (l1_ffn1_in, l1_ffn1_out))
    ffn_post = ((l0_ffn2_in, l0_ffn2_out), (l1_ffn2_in, l1_ffn2_out))
    ln_pre = ((l0_ln1_g, l0_ln1_b), (l1_ln1_g, l1_ln1_b))
    ln_mix = ((l0_ln2_g, l0_ln2_b), (l1_ln2_g, l1_ln2_b))
    ln_post = ((l0_ln3_g, l0_ln3_b), (l1_ln3_g, l1_ln3_b))
    mixer_of_layer = (
        lambda t: hgrn2_mixer(t, l0_hg_in, l0_hg_norm_g, l0_hg_out, hgrn_lower_bound(lb_logits, 0)),
        lambda t: moba_mixer(t, l1_mb_in, l1_mb_out),
    )
    h = x
    for i in range(DEPTH):
        h = layer_norm(ALPHA * h + 0.5 * swiglu_ffn(h, *ffn_pre[i]), *ln_pre[i])
        h = layer_norm(ALPHA * h + mixer_of_layer[i](h), *ln_mix[i])
        h = layer_norm(ALPHA * h + 0.5 * swiglu_ffn(h, *ffn_post[i]), *ln_post[i])
    return h
```

```python
import numpy as np
import concourse.bass as bass
import concourse.mybir as mybir
from concourse.bass_utils import run_bass_kernel_spmd

F32 = mybir.dt.float32
BF16 = mybir.dt.bfloat16
AF = mybir.ActivationFunctionType
ALU = mybir.AluOpType
AX = mybir.AxisListType

D_MODEL = 4096
SEQ = 8192
D_FF = 11008
DEPTH = 2
ALPHA = float((2 * DEPTH) ** 0.25)
LN_EPS = 1e-5
RMS_EPS = 1e-6
P = 128
TT = 512
DBG_SKIP_W = False
DBG_SKIP_MM = False


class Prog:
    ENGS = ("pe", "act", "dve", "pool", "sp")

    def __init__(self, nc):
        self.nc = nc
        self.ops = {e: [] for e in self.ENGS}
        self.cnt = {e: 0 for e in self.ENGS}
        self.chan_cnt = {}
        self.chan_order = []

    def op(self, eng, fn, deps=(), signal=True):
        tok = None
        if signal:
            self.cnt[eng] += 1
            tok = (eng, self.cnt[eng])
        self.ops[eng].append((fn, [d for d in deps if d is not None], tok))
        return tok

    def dma(self, eng, chan, fn, deps=()):
        if chan not in self.chan_cnt:
            self.chan_cnt[chan] = 0
            self.chan_order.append(chan)
        self.chan_cnt[chan] += 16
        tok = (chan, self.chan_cnt[chan])
        self.ops[eng].append((fn, [d for d in deps if d is not None], tok))
        return tok

    def emit(self, final_waits):
        nc = self.nc
        names = list(self.ENGS) + self.chan_order
        sems = {}
        import contextlib
        with contextlib.ExitStack() as st:
            for n in names:
                sems[n] = st.enter_context(nc.semaphore("s_" + n))
            block = st.enter_context(nc.Block())
            engmap = {"pe": block.tensor, "act": block.scalar, "dve": block.vector,
                      "pool": block.gpsimd, "sp": block.sync}

            def make(ename):
                ops = self.ops[ename]

                def body(eng):
                    waited = {}
                    for fn, deps, tok in ops:
                        for (src, val) in deps:
                            if waited.get(src, 0) < val:
                                eng.wait_ge(sems[src], val)
                                waited[src] = val
                        ins = fn(eng)
                        if tok is not None:
                            ins.then_inc(sems[tok[0]], 16 if tok[0] not in self.ENGS else 1)
                    if ename == "sp":
                        for (src, val) in final_waits:
                            if val > 0:
                                eng.wait_ge(sems[src], val)
                return body

            for ename in self.ENGS:
                engmap[ename](make(ename))


class Ring:
    def __init__(self, name, tiles):
        self.tiles = tiles
        self.n = len(tiles)
        self.i = 0
        self.readers = [None] * self.n
        self.name = name

    def next(self):
        k = self.i % self.n
        self.i += 1
        return k

    def chan(self, k):
        return f"{self.name}{k}"


class ProjLN:
    def __init__(self, pg, nc, es, D, FCmax, need_h, tag=""):
        self.pg, self.nc, self.D = pg, nc, D
        KC = D // P
        sb = lambda name, shape, dt: es.enter_context(nc.sbuf_tensor(name + tag, shape, dt))
        ps = lambda name, shape, dt: es.enter_context(nc.psum_tensor(name + tag, shape, dt))
        self.tag = tag
        self.hT = sb("hT", [P, KC, TT], BF16) if need_h else None
        self.actT = sb("actT", [P, FCmax, TT], BF16)
        self.wslots = [sb(f"ws{k}", [P, 32, P], BF16) for k in range(6)]
        self.wring = Ring("w" + tag, self.wslots)
        self.ones = sb("ones", [P, P], F32)
        self.gsb = sb("gsb", [P, KC], F32)
        self.bsb = sb("bsb", [P, KC], F32)
        self.silu_t = [sb(f"silu{k}", [P, TT], F32) for k in range(2)]
        self.xs = [sb(f"xs{k}", [P, TT], F32) for k in range(2)]
        self.rb = [sb(f"rb{k}", [P, TT], F32) for k in range(2)]
        self.sq = [sb(f"sq{k}", [P, TT], F32) for k in range(2)]
        self.yo = [sb(f"yo{k}", [P, TT], F32) for k in range(2)]
        self.yob = [sb(f"yob{k}", [P, TT], BF16) for k in range(2)]
        self.mean = sb("mean", [P, TT], F32)
        self.msq = sb("msq", [P, TT], F32)
        self.rstd = sb("rstd", [P, TT], F32)
        self.nmr = sb("nmr", [P, TT], F32)
        self.pa = [ps(f"pa{k}", [P, TT], F32) for k in range(4)]
        self.pb = [ps(f"pb{k}", [P, TT], F32) for k in range(2)]
        self.pst = [ps(f"pst{k}", [P, TT], F32) for k in range(2)]
        self.t_ones = pg.op("dve", lambda e: e.memset(self.ones[:], 1.0))
        self.pa_rd = [[] for _ in range(4)]
        self.pb_rd = [[] for _ in range(2)]
        self.silu_rd = [[] for _ in range(2)]
        self.xs_rd = [[] for _ in range(2)]
        self.rb_rd = [[] for _ in range(2)]
        self.sq_rd = [[] for _ in range(2)]
        self.yo_rd = [[] for _ in range(2)]
        self.yob_rd = [[] for _ in range(2)]
        self.act_rd = []
        self.hT_rd = []
        self.stat_rd = []
        self.pst_rd = []
        self.gb_rd = []

    def run(self, *, T, K2, xT, w_out, gT, bT, yT, ybT, rscr, scale, F=None, w_in=None, inT=None, in_deps=()):
        pg, D = self.pg, self.D
        tag = self.tag
        KC = D // P
        FC = K2 // P
        NT = T // TT
        hT, actT, wslots, wring, ones = self.hT, self.actT, self.wslots, self.wring, self.ones
        gsb, bsb, silu_t, xs, rb, sq, yo, yob = self.gsb, self.bsb, self.silu_t, self.xs, self.rb, self.sq, self.yo, self.yob
        mean, msq, rstd, nmr, pa, pb, pst = self.mean, self.msq, self.rstd, self.nmr, self.pa, self.pb, self.pst
        if in_deps:
            pg.op("sp", lambda e: e.nop(), deps=list(in_deps))
            pg.op("pool", lambda e: e.nop(), deps=list(in_deps))
        t_g = pg.dma("sp", "gb" + tag, lambda e: e.dma_start(out=gsb[:, 0:KC], in_=gT[:, :]), deps=self.gb_rd)
        t_b = pg.dma("sp", "gb" + tag, lambda e: e.dma_start(out=bsb[:, 0:KC], in_=bT[:, :]), deps=self.gb_rd)
        t_g = t_b
        out_tokens = []
        for tt in range(NT):
            c0 = tt * TT
            if F is not None:
                t_h = None
                nq = 4 if KC >= 4 else 1
                kq = KC // nq
                for q4 in range(nq):
                    src = xT[q4 * kq * P:(q4 + 1) * kq * P, c0:c0 + TT].rearrange("(kc p) t -> p kc t", p=P)
                    t_h = pg.dma("pool", "hT" + tag,
                                 (lambda e, src=src, q4=q4, kq=kq: e.dma_start(out=hT[:, q4 * kq:(q4 + 1) * kq, :], in_=src)),
                                 deps=self.hT_rd)
                self.hT_rd = []
                for j in range(FC):
                    toks = []
                    for which in range(2):
                        k = wring.next()
                        col = which * F + j * P
                        src = w_in[:, col:col + P].rearrange("(kc p) c -> p kc c", p=P)
                        tl = pg.dma("pool", wring.chan(k),
                                    (lambda e, k=k, src=src: e.dma_start(out=wslots[k][:, 0:(1 if DBG_SKIP_W else KC), :], in_=src[:, 0:(1 if DBG_SKIP_W else KC), :])),
                                    deps=[wring.readers[k]])
                        toks.append([k, tl, None])
                    pi = (j % 2) * 2
                    for which in range(2):
                        k, tl, _ = toks[which]
                        bank = pa[pi + which]
                        last = None
                        for kc in range(KC):
                            deps = [tl, t_h] + self.pa_rd[pi + which] if kc == 0 else []
                            last = pg.op("pe", (lambda e, bank=bank, k=k, kc=kc: e.matmul(
                                bank[:], lhsT=wslots[k][:, kc, :], rhs=hT[:, kc, :], start=(kc == 0), stop=(kc == KC - 1))),
                                deps=deps, signal=(kc == KC - 1))
                        wring.readers[k] = last
                        toks[which][2] = last
                    self.hT_rd = [toks[1][2]]
                    sk = j % 2
                    t_s = pg.op("act", (lambda e, sk=sk, bank=pa[pi]: e.activation(out=silu_t[sk][:], in_=bank[:], func=AF.Silu)),
                                deps=[toks[0][2]] + self.silu_rd[sk])
                    t_m = pg.op("dve", (lambda e, sk=sk, j=j, bank=pa[pi + 1]: e.tensor_tensor(
                        out=actT[:, j, :], in0=silu_t[sk][:], in1=bank[:], op=ALU.mult)),
                        deps=[t_s, toks[1][2]] + self.act_rd)
                    self.silu_rd[sk] = [t_m]
                    self.pa_rd[pi] = [t_s]
                    self.pa_rd[pi + 1] = [t_m]
                    act_ready = t_m
                self.act_rd = []
            else:
                act_ready = None
                nq = 4
                fq = (FC + nq - 1) // nq
                for q4 in range(nq):
                    f0, f1 = q4 * fq, min(FC, (q4 + 1) * fq)
                    if f0 >= f1:
                        continue
                    src = inT[f0 * P:f1 * P, c0:c0 + TT].rearrange("(kc p) t -> p kc t", p=P)
                    act_ready = pg.dma("sp", "actin" + tag,
                                       (lambda e, src=src, f0=f0, f1=f1: e.dma_start(out=actT[:, f0:f1, :], in_=src)),
                                       deps=self.act_rd)
                self.act_rd = []
            rs_tok = [None] * KC
            last_stat = None
            for i in range(KC):
                xk = i % 2
                srcx = xT[i * P:(i + 1) * P, c0:c0 + TT]
                t_x = pg.dma("sp", f"xs{xk}" + tag, (lambda e, xk=xk, srcx=srcx: e.dma_start(out=xs[xk][:], in_=srcx)),
                             deps=self.xs_rd[xk])
                nseg = (FC + 31) // 32
                segs = []
                for s in range(nseg):
                    f0, f1 = s * 32, min(FC, (s + 1) * 32)
                    k = wring.next()
                    src = w_out[f0 * P:f1 * P, i * P:(i + 1) * P].rearrange("(fc p) c -> p fc c", p=P)
                    tl = pg.dma("pool", wring.chan(k),
                                (lambda e, k=k, src=src, n=f1 - f0: e.dma_start(out=wslots[k][:, 0:(1 if DBG_SKIP_W else n), :], in_=src[:, 0:(1 if DBG_SKIP_W else n), :])),
                                deps=[wring.readers[k]])
                    segs.append((k, tl, f0, f1))
                bk = i % 2
                last = None
                for (k, tl, f0, f1) in segs:
                    for fc in range(f0, f1):
                        deps = []
                        if fc == f0:
                            deps = [tl]
                            if fc == 0:
                                deps += [act_ready] + self.pb_rd[bk]
                        last = pg.op("pe", (lambda e, k=k, fc=fc, f0=f0, bk=bk: e.matmul(
                            pb[bk][:], lhsT=wslots[k][:, fc - f0, :], rhs=actT[:, fc, :], start=(fc == 0), stop=(fc == FC - 1))),
                            deps=deps, signal=(fc == f1 - 1))
                    wring.readers[k] = last
                self.act_rd = [last]
                t_ax = pg.op("act", (lambda e, xk=xk: e.activation(out=xs[xk][:], in_=xs[xk][:], func=AF.Copy, scale=ALPHA)),
                             deps=[t_x])
                t_r = pg.op("dve", (lambda e, xk=xk, bk=bk: e.scalar_tensor_tensor(
                    out=rb[xk][:], in0=pb[bk][:], scalar=float(scale), in1=xs[xk][:], op0=ALU.mult, op1=ALU.add)),
                    deps=[t_ax, last] + self.rb_rd[xk])
                self.pb_rd[bk] = [t_r]
                self.xs_rd[xk] = [t_r]
                t_sq = pg.op("act", (lambda e, xk=xk: e.activation(out=sq[xk][:], in_=rb[xk][:], func=AF.Square)),
                             deps=[t_r] + self.sq_rd[xk])
                t_st0 = pg.op("pe", (lambda e, xk=xk, i=i: e.matmul(pst[0][:], lhsT=ones[:], rhs=rb[xk][:], start=(i == 0), stop=(i == KC - 1))),
                              deps=[t_r, self.t_ones] + (self.pst_rd if i == 0 else []))
                t_st1 = pg.op("pe", (lambda e, xk=xk, i=i: e.matmul(pst[1][:], lhsT=ones[:], rhs=sq[xk][:], start=(i == 0), stop=(i == KC - 1))),
                              deps=[t_sq])
                self.sq_rd[xk] = [t_st1]
                dstr = rscr[i * P:(i + 1) * P, c0:c0 + TT]
                t_rs = pg.dma("sp", f"rst{xk}" + tag, (lambda e, xk=xk, dstr=dstr: e.dma_start(out=dstr, in_=rb[xk][:])),
                              deps=[t_r])
                rs_tok[i] = t_rs
                self.rb_rd[xk] = [t_sq, t_st0, t_rs]
                last_stat = t_st1
            invD = 1.0 / D
            t_mean = pg.op("act", lambda e: e.activation(out=mean[:], in_=pst[0][:], func=AF.Copy, scale=invD),
                           deps=[last_stat] + self.stat_rd)
            t_msq = pg.op("dve", lambda e: e.tensor_tensor(out=msq[:], in0=mean[:], in1=mean[:], op=ALU.mult),
                          deps=[t_mean] + self.stat_rd)
            t_var = pg.op("dve", lambda e: e.scalar_tensor_tensor(out=msq[:], in0=pst[1][:], scalar=invD, in1=msq[:],
                                                                 op0=ALU.mult, op1=ALU.subtract), deps=[t_msq, last_stat])
            self.pst_rd = [t_var]
            t_veps = pg.op("dve", lambda e: e.tensor_scalar(out=msq[:], in0=msq[:], scalar1=float(LN_EPS), scalar2=None, op0=ALU.add),
                           deps=[t_var])
            t_std = pg.op("act", lambda e: e.activation(out=rstd[:], in_=msq[:], func=AF.Sqrt), deps=[t_veps] + self.stat_rd)
            t_rstd = pg.op("dve", lambda e: e.reciprocal(out=rstd[:], in_=rstd[:]), deps=[t_std])
            t_nmr = pg.op("dve", lambda e: e.scalar_tensor_tensor(out=nmr[:], in0=mean[:], scalar=-1.0, in1=rstd[:],
                                                                 op0=ALU.mult, op1=ALU.mult), deps=[t_rstd, t_mean] + self.stat_rd)
            for i in range(KC):
                xk = i % 2
                srcr = rscr[i * P:(i + 1) * P, c0:c0 + TT]
                t_l = pg.dma("sp", f"rld{xk}" + tag, (lambda e, xk=xk, srcr=srcr: e.dma_start(out=rb[xk][:], in_=srcr)),
                             deps=[rs_tok[i]] + self.rb_rd[xk])
                t_1 = pg.op("dve", (lambda e, xk=xk: e.tensor_tensor(out=rb[xk][:], in0=rb[xk][:], in1=rstd[:], op=ALU.mult)),
                            deps=[t_l, t_rstd])
                t_2 = pg.op("dve", (lambda e, xk=xk: e.tensor_tensor(out=rb[xk][:], in0=rb[xk][:], in1=nmr[:], op=ALU.add)),
                            deps=[t_1, t_nmr])
                t_y = pg.op("act", (lambda e, xk=xk, i=i: e.activation(out=yo[xk][:], in_=rb[xk][:], func=AF.Identity,
                                                                        scale=gsb[:, i:i + 1], bias=bsb[:, i:i + 1])),
                            deps=[t_2, t_g, t_b] + self.yo_rd[xk])
                dsty = yT[i * P:(i + 1) * P, c0:c0 + TT]
                t_ys = pg.dma("sp", f"yst{xk}" + tag, (lambda e, xk=xk, dsty=dsty: e.dma_start(out=dsty, in_=yo[xk][:])), deps=[t_y])
                out_tokens.append(t_ys)
                self.yo_rd[xk] = [t_ys]
                if ybT is not None:
                    t_yb = pg.op("dve", (lambda e, xk=xk: e.tensor_copy(out=yob[xk][:], in_=yo[xk][:])), deps=[t_y] + self.yob_rd[xk])
                    dstb = ybT[i * P:(i + 1) * P, c0:c0 + TT]
                    t_ybs = pg.dma("sp", f"ybst{xk}" + tag, (lambda e, xk=xk, dstb=dstb: e.dma_start(out=dstb, in_=yob[xk][:])), deps=[t_yb])
                    self.yob_rd[xk] = [t_ybs]
                    out_tokens.append(t_ybs)
                    self.yo_rd[xk] = [t_ys, t_yb]
                self.rb_rd[xk] = [t_y]
                self.stat_rd = [t_2]
                self.gb_rd = [t_y]
        return out_tokens


class ProjLN3:
    FCP = 22
    NHMAX = 2

    def __init__(self, pg, nc, es, D, tag=""):
        self.pg, self.nc, self.D = pg, nc, D
        KC = D // P
        sb = lambda name, shape, dt: es.enter_context(nc.sbuf_tensor(name + tag, shape, dt))
        ps = lambda name, shape, dt: es.enter_context(nc.psum_tensor(name + tag, shape, dt))
        self.tag = tag
        TW = self.NHMAX * TT
        self.hT = sb("hT", [P, KC, TW], BF16)
        self.actT = sb("actT", [P, self.FCP, TW], BF16)
        self.wslots = [sb(f"ws{k}", [P, 32, P], BF16) for k in range(6)]
        self.wring = Ring("w" + tag, self.wslots)
        self.ones = sb("ones", [P, P], F32)
        self.gsb = sb("gsb", [P, KC], F32)
        self.bsb = sb("bsb", [P, KC], F32)
        self.silu_t = [sb(f"silu{k}", [P, TT], F32) for k in range(2)]
        self.xs = [sb(f"xs{k}", [P, TT], F32) for k in range(2)]
        self.rb = [sb(f"rb{k}", [P, TT], F32) for k in range(2)]
        self.sq = [sb(f"sq{k}", [P, TT], F32) for k in range(2)]
        self.yo = [sb(f"yo{k}", [P, TT], F32) for k in range(2)]
        self.yob = [sb(f"yob{k}", [P, TT], BF16) for k in range(2)]
        self.mean = sb("mean", [P, TT], F32)
        self.msq = sb("msq", [P, TT], F32)
        self.rstd = sb("rstd", [P, TT], F32)
        self.nmr = sb("nmr", [P, TT], F32)
        self.pa = [ps(f"pa{k}", [P, TT], F32) for k in range(4)]
        self.pb = [ps(f"pb{k}", [P, TT], F32) for k in range(2)]
        self.pst = [ps(f"pst{k}", [P, TT], F32) for k in range(2)]
        self.t_ones = pg.op("dve", lambda e: e.memset(self.ones[:], 1.0))
        z = lambda n: [[] for _ in range(n)]
        self.pa_rd, self.pb_rd, self.pst_rd = z(4), z(2), z(2)
        self.silu_rd, self.xs_rd, self.rb_rd, self.sq_rd, self.yo_rd, self.yob_rd = z(2), z(2), z(2), z(2), z(2), z(2)
        self.act_rd = []
        self.hT_rd = []
        self.stat_rd = []
        self.gb_rd = []

    def run(self, *, T, K2, xT, w_out, gT, bT, yT, ybT, rscr, scale, F=None, w_in=None, inT=None, in_deps=()):
        pg, D, tag = self.pg, self.D, self.tag
        KC = D // P
        FC = K2 // P
        hT, actT, wslots, wring, ones = self.hT, self.actT, self.wslots, self.wring, self.ones
        gsb, bsb, silu_t, xs, rb, sq, yo, yob = self.gsb, self.bsb, self.silu_t, self.xs, self.rb, self.sq, self.yo, self.yob
        mean, msq, rstd, nmr, pa, pb, pst = self.mean, self.msq, self.rstd, self.nmr, self.pa, self.pb, self.pst
        if in_deps:
            pg.op("sp", lambda e: e.nop(), deps=list(in_deps))
            pg.op("pool", lambda e: e.nop(), deps=list(in_deps))
        t_g = pg.dma("sp", "gb" + tag, lambda e: e.dma_start(out=gsb[:, 0:KC], in_=gT[:, :]), deps=self.gb_rd)
        t_b = pg.dma("sp", "gb" + tag, lambda e: e.dma_start(out=bsb[:, 0:KC], in_=bT[:, :]), deps=self.gb_rd)
        t_g = t_b
        out_tokens = []
        TW = self.NHMAX * TT
        for c0 in range(0, T, TW):
            W = min(TW, T - c0)
            NH = W // TT
            nq = 4 if KC >= 4 else 1
            kq = KC // nq
            t_h = None
            for q4 in range(nq):
                if F is not None:
                    src = xT[q4 * kq * P:(q4 + 1) * kq * P, c0:c0 + W].rearrange("(kc p) t -> p kc t", p=P)
                    t_h = pg.dma("pool", "hT" + tag,
                                 (lambda e, src=src, q4=q4, kq=kq, W=W: e.dma_start(out=hT[:, q4 * kq:(q4 + 1) * kq, 0:W], in_=src)),
                                 deps=self.hT_rd)
                else:
                    src = inT[q4 * kq * P:(q4 + 1) * kq * P, c0:c0 + W].rearrange("(kc p) t -> p kc t", p=P)
                    t_h = pg.dma("sp", "hT" + tag,
                                 (lambda e, src=src, q4=q4, kq=kq, W=W: e.dma_start(out=hT[:, q4 * kq:(q4 + 1) * kq, 0:W], in_=src)),
                                 deps=self.hT_rd)
            self.hT_rd = []
            if F is not None:
                parts = [(f0, min(FC, f0 + self.FCP)) for f0 in range(0, FC, self.FCP)]
            else:
                parts = [(0, FC)]
            nparts = len(parts)
            rs_tok = {}
            last_stat = [None] * NH
            for pi_, (pf0, pf1) in enumerate(parts):
                lastp = pi_ == nparts - 1
                if F is not None:
                    for j in range(pf0, pf1):
                        toks = []
                        for which in range(2):
                            k = wring.next()
                            col = which * F + j * P
                            src = w_in[:, col:col + P].rearrange("(kc p) c -> p kc c", p=P)
                            tl = pg.dma("pool", wring.chan(k),
                                        (lambda e, k=k, src=src: e.dma_start(out=wslots[k][:, 0:(1 if DBG_SKIP_W else KC), :], in_=src[:, 0:(1 if DBG_SKIP_W else KC), :])),
                                        deps=[wring.readers[k]])
                            toks.append((k, tl))
                        for half in range(NH):
                            hs = slice(half * TT, (half + 1) * TT)
                            mm = []
                            for which in range(2):
                                k, tl = toks[which]
                                bi = half * 2 + which
                                bank = pa[bi]
                                last = None
                                for kc in range(KC):
                                    deps = ([tl, t_h] + self.pa_rd[bi]) if kc == 0 else []
                                    last = pg.op("pe", (lambda e, bank=bank, k=k, kc=kc, hs=hs: e.matmul(
                                        bank[:], lhsT=wslots[k][:, kc, :], rhs=hT[:, kc, hs], start=(kc == 0), stop=(kc == KC - 1))),
                                        deps=deps, signal=(kc == KC - 1))
                                self.pa_rd[bi] = []
                                wring.readers[k] = last
                                mm.append(last)
                            self.hT_rd = [mm[1]]
                            sk = half
                            t_s = pg.op("act", (lambda e, sk=sk, bank=pa[half * 2]: e.activation(out=silu_t[sk][:], in_=bank[:], func=AF.Silu)),
                                        deps=[mm[0]] + self.silu_rd[sk])
                            t_m = pg.op("dve", (lambda e, sk=sk, jj=j - pf0, hs=hs, bank=pa[half * 2 + 1]: e.tensor_tensor(
                                out=actT[:, jj, hs], in0=silu_t[sk][:], in1=bank[:], op=ALU.mult)),
                                deps=[t_s, mm[1]] + self.act_rd)
                            self.silu_rd[sk] = [t_m]
                            self.pa_rd[half * 2] = [t_s]
                            self.pa_rd[half * 2 + 1] = [t_m]
                            act_ready = t_m
                    self.act_rd = []
                    src_act = actT
                else:
                    act_ready = t_h
                    src_act = hT
                nfc = pf1 - pf0
                for i in range(KC):
                    k = wring.next()
                    src = w_out[pf0 * P:pf1 * P, i * P:(i + 1) * P].rearrange("(fc p) c -> p fc c", p=P)
                    tl = pg.dma("pool", wring.chan(k),
                                (lambda e, k=k, src=src, n=nfc: e.dma_start(out=wslots[k][:, 0:(1 if DBG_SKIP_W else n), :], in_=src[:, 0:(1 if DBG_SKIP_W else n), :])),
                                deps=[wring.readers[k]])
                    for half in range(NH):
                        hs = slice(half * TT, (half + 1) * TT)
                        xk = half
                        cs = slice(c0 + half * TT, c0 + (half + 1) * TT)
                        if pi_ == 0:
                            srcx = xT[i * P:(i + 1) * P, cs]
                            xdeps = self.xs_rd[xk]
                        else:
                            srcx = rscr[i * P:(i + 1) * P, cs]
                            xdeps = self.xs_rd[xk] + [rs_tok[(i, half)]]
                        t_x = pg.dma("sp", f"xs{xk}" + tag, (lambda e, xk=xk, srcx=srcx: e.dma_start(out=xs[xk][:], in_=srcx)), deps=xdeps)
                        last = None
                        for fc in range(nfc):
                            deps = ([tl, act_ready] + self.pb_rd[half]) if fc == 0 else []
                            last = pg.op("pe", (lambda e, k=k, fc=fc, half=half, hs=hs, src_act=src_act, nfc=nfc: e.matmul(
                                pb[half][:], lhsT=wslots[k][:, fc, :], rhs=src_act[:, fc, hs], start=(fc == 0), stop=(fc == nfc - 1))),
                                deps=deps, signal=(fc == nfc - 1))
                        wring.readers[k] = last
                        if F is not None:
                            self.act_rd = [last]
                        else:
                            self.hT_rd = [last]
                        if pi_ == 0:
                            t_ax = pg.op("act", (lambda e, xk=xk: e.activation(out=xs[xk][:], in_=xs[xk][:], func=AF.Copy, scale=ALPHA)), deps=[t_x])
                        else:
                            t_ax = t_x
                        t_r = pg.op("dve", (lambda e, xk=xk, half=half: e.scalar_tensor_tensor(
                            out=rb[xk][:], in0=pb[half][:], scalar=float(scale), in1=xs[xk][:], op0=ALU.mult, op1=ALU.add)),
                            deps=[t_ax, last] + self.rb_rd[xk])
                        self.pb_rd[half] = [t_r]
                        self.xs_rd[xk] = [t_r]
                        dstr = rscr[i * P:(i + 1) * P, cs]
                        t_rs = pg.dma("sp", f"rst{xk}" + tag, (lambda e, xk=xk, dstr=dstr: e.dma_start(out=dstr, in_=rb[xk][:])), deps=[t_r])
                        rs_tok[(i, half)] = t_rs
                        self.rb_rd[xk] = [t_rs]
                        if lastp:
                            t_sq = pg.op("act", (lambda e, xk=xk: e.activation(out=sq[xk][:], in_=rb[xk][:], func=AF.Square)),
                                         deps=[t_r] + self.sq_rd[xk])
                            t_st0 = pg.op("pe", (lambda e, xk=xk, i=i, half=half: e.matmul(pst[half][:], lhsT=ones[:], rhs=rb[xk][:], start=(i == 0), stop=(i == KC - 1))),
                                          deps=[t_r, self.t_ones] + (self.pst_rd[half] if i == 0 else []))
                            t_st1 = pg.op("pe", (lambda e, xk=xk, i=i, half=half: e.matmul(pa[half][:], lhsT=ones[:], rhs=sq[xk][:], start=(i == 0), stop=(i == KC - 1))),
                                          deps=[t_sq] + (self.pa_rd[half] if i == 0 else []))
                            if i == 0:
                                self.pst_rd[half] = []
                                self.pa_rd[half] = []
                            self.sq_rd[xk] = [t_st1]
                            self.rb_rd[xk] = [t_sq, t_st0, t_rs]
                            last_stat[half] = t_st1
            invD = 1.0 / D
            for half in range(NH):
                cs = slice(c0 + half * TT, c0 + (half + 1) * TT)
                t_mean = pg.op("act", (lambda e, half=half: e.activation(out=mean[:], in_=pst[half][:], func=AF.Copy, scale=invD)),
                               deps=[last_stat[half]] + self.stat_rd)
                t_msq = pg.op("dve", lambda e: e.tensor_tensor(out=msq[:], in0=mean[:], in1=mean[:], op=ALU.mult),
                              deps=[t_mean] + self.stat_rd)
                t_var = pg.op("dve", (lambda e, half=half: e.scalar_tensor_tensor(out=msq[:], in0=pa[half][:], scalar=invD, in1=msq[:],
                                                                                   op0=ALU.mult, op1=ALU.subtract)), deps=[t_msq, last_stat[half]])
                self.pst_rd[half] = [t_mean]
                self.pa_rd[half] = [t_var]
                t_veps = pg.op("dve", lambda e: e.tensor_scalar(out=msq[:], in0=msq[:], scalar1=float(LN_EPS), scalar2=None, op0=ALU.add),
                               deps=[t_var])
                t_std = pg.op("act", lambda e: e.activation(out=rstd[:], in_=msq[:], func=AF.Sqrt), deps=[t_veps] + self.stat_rd)
                t_rstd = pg.op("dve", lambda e: e.reciprocal(out=rstd[:], in_=rstd[:]), deps=[t_std])
                t_nmr = pg.op("dve", lambda e: e.scalar_tensor_tensor(out=nmr[:], in0=mean[:], scalar=-1.0, in1=rstd[:],
                                                                     op0=ALU.mult, op1=ALU.mult), deps=[t_rstd, t_mean] + self.stat_rd)
                for i in range(KC):
                    xk = i % 2
                    srcr = rscr[i * P:(i + 1) * P, cs]
                    t_l = pg.dma("sp", f"rld{xk}" + tag, (lambda e, xk=xk, srcr=srcr: e.dma_start(out=rb[xk][:], in_=srcr)),
                                 deps=[rs_tok[(i, half)]] + self.rb_rd[xk])
                    t_1 = pg.op("dve", (lambda e, xk=xk: e.tensor_tensor(out=rb[xk][:], in0=rb[xk][:], in1=rstd[:], op=ALU.mult)),
                                deps=[t_l, t_rstd])
                    t_2 = pg.op("dve", (lambda e, xk=xk: e.tensor_tensor(out=rb[xk][:], in0=rb[xk][:], in1=nmr[:], op=ALU.add)),
                                deps=[t_1, t_nmr])
                    t_y = pg.op("act", (lambda e, xk=xk, i=i: e.activation(out=yo[xk][:], in_=rb[xk][:], func=AF.Identity,
                                                                            scale=gsb[:, i:i + 1], bias=bsb[:, i:i + 1])),
                                deps=[t_2, t_g, t_b] + self.yo_rd[xk])
                    dsty = yT[i * P:(i + 1) * P, cs]
                    t_ys = pg.dma("sp", f"yst{xk}" + tag, (lambda e, xk=xk, dsty=dsty: e.dma_start(out=dsty, in_=yo[xk][:])), deps=[t_y])
                    out_tokens.append(t_ys)
                    self.yo_rd[xk] = [t_ys]
                    if ybT is not None:
                        t_yb = pg.op("dve", (lambda e, xk=xk: e.tensor_copy(out=yob[xk][:], in_=yo[xk][:])), deps=[t_y] + self.yob_rd[xk])
                        dstb = ybT[i * P:(i + 1) * P, cs]
                        t_ybs = pg.dma("sp", f"ybst{xk}" + tag, (lambda e, xk=xk, dstb=dstb: e.dma_start(out=dstb, in_=yob[xk][:])), deps=[t_yb])
                        self.yob_rd[xk] = [t_ybs]
                        out_tokens.append(t_ybs)
                        self.yo_rd[xk] = [t_ys, t_yb]
                    self.rb_rd[xk] = [t_y]
                    self.stat_rd = [t_2]
                    self.gb_rd = [t_y]
        return out_tokens


NEG = -30000.0


def emit_moba(pg, nc, es, *, D, S, HPC, hb, wqkv, consts, oT, dbg=3):
    KC = D // P
    L = 256
    NBLK = S // L
    NQT = S // P
    NTB = S // TT
    scale = float(P) ** -0.5
    NPM = 5
    SK1, SK2 = 2, 4
    sb = lambda name, shape, dt: es.enter_context(nc.sbuf_tensor("mb_" + name, shape, dt))
    ps = lambda name, shape, dt: es.enter_context(nc.psum_tensor("mb_" + name, shape, dt))
    hTb = [sb(f"hTb{k}", [P, KC, TT], BF16) for k in range(2)]
    wsb = sb("wsb", [P, 3, KC, P], BF16)
    QT = sb("QT", [P, S], BF16)
    KT = sb("KT", [P, S], BF16)
    Vsb = sb("Vsb", [P, S // P, P], BF16)
    kmean_f = sb("kmean_f", [P, NBLK], F32)
    kmean_b = sb("kmean_b", [P, NBLK], BF16)
    identb = sb("identb", [P, P], BF16)
    identf = sb("identf", [P, P], F32)
    triM = sb("triM", [P, P], BF16)
    MB = sb("MB", [P, NBLK, NBLK], F32)
    gate_sb = [sb(f"gate{k}", [P, NBLK], F32) for k in range(2)]
    top8 = [sb(f"top8{k}", [P, 8], F32) for k in range(2)]
    selb = sb("selb", [P, NQT, NBLK], F32)
    Pm = [sb(f"Pm{k}", [P, L], BF16) for k in range(NPM)]
    PTsb = [sb(f"PTsb{k}", [P, 2, P], BF16) for k in range(NPM)]
    NV = NBLK + 8
    rs = [sb(f"rs{k}", [P, NV], F32) for k in range(8)]
    rsum = [sb(f"rsum{k}", [P, 1], F32) for k in range(2)]
    Osb = [sb(f"Osb{k}", [P, P], F32) for k in range(2)]
    oT_sb = sb("oT_sb", [P, S], BF16)
    B = [ps(f"B{k}", [P, TT], F32) for k in range(3)]
    O = [ps(f"O{k}", [P, TT], F32) for k in range(2)]
    PTv = [ps(f"PT{k}", [P, 8, P], BF16) for k in range(2)]

    t_c = pg.dma("sp", "mbc", lambda e: e.dma_start(out=identf[:], in_=consts[:, 0:P]))
    t_c = pg.dma("sp", "mbc", lambda e: e.dma_start(out=MB[:], in_=consts[:, 2 * P:2 * P + NBLK * NBLK].rearrange("p (j n) -> p j n", n=NBLK)))
    t_c2 = pg.dma("pool", "mbc2", lambda e: e.dma_start(out=identb[:], in_=consts[:, 0:P]))
    t_c2 = pg.dma("pool", "mbc2", lambda e: e.dma_start(out=triM[:], in_=consts[:, P:2 * P]))

    B_rd = [[], [], []]
    hT_rd = [[], []]
    w_rd = []
    qkv_rd = []
    O_rd = [[], []]
    PT_rd = [[], []]
    Pm_rd = [[] for _ in range(NPM)]
    PTsb_rd = [[] for _ in range(NPM)]
    rs_rd = [[] for _ in range(8)]
    Osb_rd = [[], []]
    oTsb_rd = []
    out_tokens = []
    hcount = 0
    for h in range(HPC):
        t_w = None
        for which in range(3):
            src = wqkv[which, :, h * P:(h + 1) * P].rearrange("(kc p) c -> p kc c", p=P)
            t_w = pg.dma("pool", "mbw", (lambda e, which=which, src=src: e.dma_start(out=wsb[:, which, :, :], in_=src)), deps=w_rd)
        w_rd = []
        kred = []
        for tb in range(NTB):
            hk = hcount % 2
            hcount += 1
            t_h = None
            nq = 4 if KC >= 4 else 1
            kq = KC // nq
            for q4 in range(nq):
                src = hb[q4 * kq * P:(q4 + 1) * kq * P, tb * TT:(tb + 1) * TT].rearrange("(kc p) t -> p kc t", p=P)
                t_h = pg.dma("sp", f"mbh{hk}", (lambda e, hk=hk, src=src, q4=q4, kq=kq: e.dma_start(out=hTb[hk][:, q4 * kq:(q4 + 1) * kq, :], in_=src)),
                             deps=hT_rd[hk])
            hT_rd[hk] = []
            for which, dst in ((0, QT), (1, KT)):
                if dbg == 11 and which == 1:
                    continue
                bank = B[which]
                last = None
                for kc in range(KC):
                    deps = ([t_w, t_h] + B_rd[which] + qkv_rd) if kc == 0 else []
                    last = pg.op("pe", (lambda e, bank=bank, which=which, kc=kc, hk=hk: e.matmul(
                        bank[:], lhsT=wsb[:, which, kc, :], rhs=hTb[hk][:, kc, :], start=(kc == 0), stop=(kc == KC - 1))),
                        deps=deps, signal=(kc == KC - 1))
                if which == 0:
                    t_cp = pg.op("act", (lambda e, bank=bank, dst=dst, tb=tb: e.activation(out=dst[:, tb * TT:(tb + 1) * TT], in_=bank[:], func=AF.Copy)),
                                 deps=[last] + qkv_rd)
                else:
                    for bl in range(TT // L):
                        t_cp = pg.op("act", (lambda e, bank=bank, dst=dst, tb=tb, bl=bl: e.activation(
                            out=dst[:, tb * TT + bl * L:tb * TT + (bl + 1) * L], in_=bank[:, bl * L:(bl + 1) * L], func=AF.Copy,
                            accum_out=kmean_f[:, tb * 2 + bl:tb * 2 + bl + 1])), deps=[last] + qkv_rd)
                    kred.append(t_cp)
                rd = [t_cp]
                B_rd[which] = rd
            last = None
            if dbg in (11, 12, 13):
                hT_rd[hk] = [t_cp]
                w_rd = [t_cp]
                proj_done = [t_cp]
                continue
            for c in range(TT // P):
                for kc in range(KC):
                    deps = (B_rd[2] + qkv_rd) if (kc == 0 and c == 0) else []
                    last = pg.op("pe", (lambda e, c=c, kc=kc, hk=hk: e.matmul(
                        B[2][:, c * P:(c + 1) * P], lhsT=hTb[hk][:, kc, c * P:(c + 1) * P], rhs=wsb[:, 2, kc, :],
                        start=(kc == 0), stop=(kc == KC - 1))),
                        deps=deps, signal=(kc == KC - 1 and c == TT // P - 1))
            t_v = pg.op("dve", (lambda e, tb=tb: e.tensor_copy(out=Vsb[:, tb * 4:(tb + 1) * 4, :], in_=B[2][:].rearrange("p (c e) -> p c e", e=P))),
                        deps=[last] + qkv_rd)
            B_rd[2] = [t_v]
            hT_rd[hk] = [last]
            w_rd = [last]
            proj_done = [t_cp, t_v]
        qkv_rd = []
        if dbg in (1, 11, 12, 13):
            t_o = pg.dma("sp", "mbo", (lambda e, h=h: e.dma_start(out=oT[h * P:(h + 1) * P, :], in_=QT[:, :])), deps=proj_done)
            out_tokens.append(t_o)
            qkv_rd = [t_o]
            continue
        t_kb = pg.op("dve", lambda e: e.tensor_scalar(out=kmean_b[:], in0=kmean_f[:], scalar1=1.0 / L, scalar2=None, op0=ALU.mult),
                     deps=kred)
        sel_done = None
        gbanks = [B[0], B[1], O[0], O[1]]
        gnames = [(B_rd, 0), (B_rd, 1), (O_rd, 0), (O_rd, 1)]
        QPB = TT // NBLK
        assert NQT <= 4 * QPB
        t_gm = [None] * NQT
        for qt in range(NQT):
            gb, gi = qt // QPB, qt % QPB
            lst, li = gnames[gb]
            t_gm[qt] = pg.op("pe", (lambda e, qt=qt, gb=gb, gi=gi: e.matmul(gbanks[gb][:, gi * NBLK:(gi + 1) * NBLK], lhsT=QT[:, qt * P:(qt + 1) * P],
                                                                            rhs=kmean_b[:, :], start=True, stop=True)),
                               deps=([t_kb] + proj_done + lst[li]) if gi == 0 else [])
        for qt in range(NQT):
            j = qt // 2
            gk = qt % 2
            gb, gi = qt // QPB, qt % QPB
            t_gs = pg.op("dve", (lambda e, gk=gk, j=j, gb=gb, gi=gi: e.tensor_tensor(out=gate_sb[gk][:], in0=gbanks[gb][:, gi * NBLK:(gi + 1) * NBLK],
                                                                                      in1=MB[:, j, :], op=ALU.add)),
                         deps=[t_gm[min(NQT - 1, (gb + 1) * QPB - 1)], t_c])
            lst, li = gnames[gb]
            lst[li] = [t_gs]
            t_t8 = pg.op("dve", (lambda e, gk=gk: e.max(out=top8[gk][:], in_=gate_sb[gk][:])), deps=[t_gs])
            t_s1 = pg.op("dve", (lambda e, gk=gk, qt=qt: e.tensor_scalar(out=selb[:, qt, :], in0=gate_sb[gk][:], scalar1=top8[gk][:, 2:3], scalar2=1.0,
                                                                          op0=ALU.is_ge, op1=ALU.subtract)), deps=[t_t8])
            sel_done = pg.op("dve", (lambda e, qt=qt: e.tensor_scalar(out=selb[:, qt, :], in0=selb[:, qt, :], scalar1=-NEG, scalar2=None, op0=ALU.mult)),
                             deps=[t_s1])
        if dbg == 2:
            t_o = pg.dma("sp", "mbo", (lambda e, h=h: e.dma_start(out=oT[h * P:(h + 1) * P, :], in_=QT[:, :])), deps=proj_done + [sel_done])
            out_tokens.append(t_o)
            qkv_rd = [t_o]
            continue
        visits = []
        for qt in range(NQT):
            j, half = qt // 2, qt % 2
            vl = []
            for n in range(j):
                vl.append(("past", n, 2 * n, 2))
            if half == 0:
                vl.append(("diag", j, 2 * j, 1))
            else:
                vl.append(("full", j, 2 * j, 1))
                vl.append(("diag", j, 2 * j + 1, 1))
            for idx, v in enumerate(vl):
                visits.append((qt, idx == 0, idx == len(vl) - 1, v[0], v[1], v[2], v[3], idx))
        nv = len(visits)
        st = [dict() for _ in range(nv)]

        def do_qk(s):
            qt, first, lastv, kind, n, ch0, nch, idx = visits[s]
            bk = s % 2
            W = nch * P
            t = pg.op("pe", (lambda e: e.matmul(B[bk][:, 0:W], lhsT=QT[:, qt * P:(qt + 1) * P], rhs=KT[:, ch0 * P:ch0 * P + W],
                                                start=True, stop=(kind != "diag"))),
                      deps=B_rd[bk] + [sel_done] + proj_done, signal=(kind != "diag"))
            if kind == "diag":
                t = pg.op("pe", (lambda e: e.matmul(B[bk][:, 0:W], lhsT=identb[:], rhs=triM[:], start=False, stop=True)), deps=[t_c2])
            st[s]["qk"] = t

        def do_exp(s):
            qt, first, lastv, kind, n, ch0, nch, idx = visits[s]
            bk = s % 2
            pk = s % NPM
            rk = qt % 8
            W = nch * P
            if first:
                st[s]["rsz"] = pg.op("dve", (lambda e: e.memset(rs[rk][:], 0.0)), deps=rs_rd[rk])
                rs_rd[rk] = []
            deps = [st[s]["qk"]] + Pm_rd[pk] + ([st[s]["rsz"]] if first else [])
            if kind == "past":
                t = pg.op("act", (lambda e: e.activation(out=Pm[pk][:, 0:W], in_=B[bk][:, 0:W], func=AF.Exp, scale=scale,
                                                         bias=selb[:, qt, n:n + 1], accum_out=rs[rk][:, idx:idx + 1])), deps=deps)
            else:
                t = pg.op("act", (lambda e: e.activation(out=Pm[pk][:, 0:W], in_=B[bk][:, 0:W], func=AF.Exp, scale=scale,
                                                         accum_out=rs[rk][:, idx:idx + 1])), deps=deps)
            B_rd[bk] = [t]
            st[s]["exp"] = t

        def do_tr(s):
            qt, first, lastv, kind, n, ch0, nch, idx = visits[s]
            pk = s % NPM
            tk = s % 2
            t = None
            for c in range(nch):
                t = pg.op("pe", (lambda e, c=c: e.transpose(PTv[tk][:, c, :], Pm[pk][:, c * P:(c + 1) * P], identb[:])),
                          deps=([st[s]["exp"], t_c2] + PT_rd[tk]) if c == 0 else [], signal=(c == nch - 1))
            Pm_rd[pk] = [t]
            t2 = pg.op("dve", (lambda e: e.tensor_copy(out=PTsb[pk][:, 0:nch, :], in_=PTv[tk][:, 0:nch, :])), deps=[t] + PTsb_rd[pk])
            PT_rd[tk] = [t2]
            st[s]["cp"] = t2

        def do_pv(s):
            qt, first, lastv, kind, n, ch0, nch, idx = visits[s]
            pk = s % NPM
            ok = qt % 2
            t = None
            for c in range(nch):
                fst = first and c == 0
                lst = lastv and c == nch - 1
                t = pg.op("pe", (lambda e, c=c, fst=fst, lst=lst: e.matmul(O[ok][:, 0:P], lhsT=PTsb[pk][:, c, :], rhs=Vsb[:, ch0 + c, :],
                                                                            start=fst, stop=lst)),
                          deps=([st[s]["cp"]] + (O_rd[ok] if fst else [])) if c == 0 else [], signal=(c == nch - 1))
            PTsb_rd[pk] = [t]
            if lastv:
                rk = qt % 2
                rq = qt % 8
                t_rs = pg.op("dve", (lambda e: e.tensor_scalar(out=rs[rq][:], in0=rs[rq][:], scalar1=1.0, scalar2=0.0, op0=ALU.mult, op1=ALU.add,
                                                               accum_out=rsum[rk][:])), deps=[st[s]["exp"]])
                t_ri = pg.op("dve", (lambda e: e.reciprocal(out=rsum[rk][:], in_=rsum[rk][:])), deps=[t_rs])
                rs_rd[rq] = [t_rs]
                t_on = pg.op("dve", (lambda e: e.tensor_scalar(out=Osb[ok][:], in0=O[ok][:, 0:P], scalar1=rsum[rk][:, 0:1], scalar2=None, op0=ALU.mult)),
                             deps=[t, t_ri] + Osb_rd[ok])
                O_rd[ok] = [t_on]
                t_tr = pg.op("pe", (lambda e: e.transpose(B[2][:, 0:P], Osb[ok][:], identf[:])), deps=[t_on, t_c] + B_rd[2])
                Osb_rd[ok] = [t_tr]
                t_oc = pg.op("act", (lambda e: e.activation(out=oT_sb[:, qt * P:(qt + 1) * P], in_=B[2][:, 0:P], func=AF.Copy)),
                             deps=[t_tr] + (oTsb_rd if qt == 0 else []))
                B_rd[2] = [t_oc]
                st[s]["fin"] = t_oc

        for s in range(nv + SK2):
            if s < nv:
                do_qk(s)
                do_exp(s)
            if 0 <= s - SK1 < nv:
                do_tr(s - SK1)
            if 0 <= s - SK2 < nv:
                do_pv(s - SK2)
        fin = st[nv - 1]["fin"]
        qkv_rd = [fin, st[nv - 1]["qk"]]
        t_o = pg.dma("sp", "mbo", (lambda e, h=h: e.dma_start(out=oT[h * P:(h + 1) * P, :], in_=oT_sb[:, :])), deps=[fin])
        oTsb_rd = [t_o]
        out_tokens.append(t_o)
    return out_tokens


def prog_check(pg):
    done = {}
    pos = {e: 0 for e in pg.ENGS}
    progress = True
    while progress:
        progress = False
        for e in pg.ENGS:
            ops = pg.ops[e]
            while pos[e] < len(ops):
                fn, deps, tok = ops[pos[e]]
                if all(done.get(src, 0) >= val for (src, val) in deps):
                    if tok is not None:
                        done[tok[0]] = max(done.get(tok[0], 0), tok[1])
                    pos[e] += 1
                    progress = True
                else:
                    break
    stuck = {e: (pos[e], len(pg.ops[e])) for e in pg.ENGS if pos[e] < len(pg.ops[e])}
    for e, (p_, n) in stuck.items():
        fn, deps, tok = pg.ops[e][p_]
        print("STUCK", e, p_, n, "deps", deps, "have", {s: done.get(s, 0) for s, _ in deps})
    return not stuck


def emit_hgrn(pg, nc, es, *, D, S, HPC, hb, w4, hgp, consts, oT):
    KC = D // P
    C = 64
    NCH = TT // C
    NTB = S // TT
    sb = lambda name, shape, dt: es.enter_context(nc.sbuf_tensor("hg_" + name, shape, dt))
    ps = lambda name, shape, dt: es.enter_context(nc.psum_tensor("hg_" + name, shape, dt))
    hTb = [sb(f"hTb{k}", [P, KC, TT], BF16) for k in range(2)]
    wsb = sb("wsb", [P, 4, KC, P], BF16)
    hgp_sb = sb("hgp", [P, HPC, 4], F32)
    lbx = sb("lbx", [P, HPC, 3], F32)
    lbm = sb("lbm", [P, HPC], F32)
    lbs = sb("lbs", [P, HPC], F32)
    lb = sb("lb", [P, HPC], F32)
    oml = sb("oml", [P, HPC], F32)
    identb = sb("identb", [P, P], BF16)
    onesf = sb("onesf", [P, P], F32)
    ones64 = sb("ones64", [P, C], F32)
    maskT = sb("maskT", [P, C], F32)
    f32t = lambda n: sb(n, [P, TT], F32)
    qs, sg, fg, lf, kk, sgg, cum, e1, e2, e3, e4, osq, ms, t1 = [f32t(n) for n in
        ("qs", "sg", "fg", "lf", "kk", "sgg", "cum", "e1", "e2", "e3", "e4", "osq", "ms", "t1")]
    negmid = sb("negmid", [P, NCH], F32)
    el = sb("el", [P, NCH], F32)
    qt_bf, kt_bf, qh_bf, kh_bf, vT_bf = [sb(n, [P, TT], BF16) for n in ("qt_bf", "kt_bf", "qh_bf", "kh_bf", "vT_bf")]
    khT_sb = sb("khT_sb", [P, NCH, P], BF16)
    vtm_sb = sb("vtm_sb", [P, NCH, P], BF16)
    scm = [sb(f"scm{k}", [P, C], BF16) for k in range(2)]
    Sst = [sb(f"S{k}", [P, P], F32) for k in range(2)]
    S_bf = [sb(f"Sbf{k}", [P, P], BF16) for k in range(2)]
    out_sb = sb("out_sb", [P, S], BF16)
    PA = ps("PA", [P, TT], F32)
    PB = ps("PB", [P, TT], F32)
    TR = [ps(f"TR{k}", [P, NCH, P], BF16) for k in range(2)]
    Ob = ps("Ob", [P, TT], F32)
    SC = ps("SC", [P, TT], F32)
    KV = [ps(f"KV{k}", [P, TT], F32) for k in range(2)]
    cum3 = cum[:].rearrange("p (c t) -> p c t", t=C)

    t_c1 = pg.dma("pool", "hgc", lambda e: e.dma_start(out=identb[:], in_=consts[:, 0:P]))
    t_c2 = pg.dma("sp", "hgc2", lambda e: e.dma_start(out=maskT[:], in_=consts[:, P:P + C]))
    t_c3 = pg.dma("sp", "hgc3", lambda e: e.dma_start(out=hgp_sb[:], in_=hgp[:, :].rearrange("p (h f) -> p h f", f=4)))
    t_o1 = pg.op("dve", lambda e: e.memset(onesf[:], 1.0))
    t_o2 = pg.op("dve", lambda e: e.memset(ones64[:], 1.0))
    t_z = pg.op("dve", lambda e: e.memset(khT_sb[:], 0.0))
    t_z = pg.op("dve", lambda e: e.memset(vtm_sb[:], 0.0))
    t_z = pg.op("dve", lambda e: e.memset(scm[0][:], 0.0))
    t_z = pg.op("dve", lambda e: e.memset(scm[1][:], 0.0))
    t = pg.op("dve", lambda e: e.tensor_tensor(out=lbm[:], in0=hgp_sb[:, :, 0], in1=hgp_sb[:, :, 1], op=ALU.max), deps=[t_c3])
    t = pg.op("dve", lambda e: e.tensor_tensor(out=lbm[:], in0=lbm[:], in1=hgp_sb[:, :, 2], op=ALU.max), deps=[t])
    for r in range(3):
        t = pg.op("dve", (lambda e, r=r: e.tensor_tensor(out=lbx[:, :, r], in0=hgp_sb[:, :, r], in1=lbm[:], op=ALU.subtract)), deps=[t])
    t = pg.op("act", lambda e: e.activation(out=lbx[:], in_=lbx[:], func=AF.Exp), deps=[t])
    t = pg.op("dve", lambda e: e.tensor_tensor(out=lbs[:], in0=lbx[:, :, 0], in1=lbx[:, :, 1], op=ALU.add), deps=[t])
    t = pg.op("dve", lambda e: e.tensor_tensor(out=lbs[:], in0=lbs[:], in1=lbx[:, :, 2], op=ALU.add), deps=[t])
    t = pg.op("dve", lambda e: e.reciprocal(out=lbs[:], in_=lbs[:]), deps=[t])
    t = pg.op("dve", lambda e: e.tensor_tensor(out=lb[:], in0=lbx[:, :, 0], in1=lbs[:], op=ALU.mult), deps=[t])
    t_lb = pg.op("dve", lambda e: e.tensor_scalar(out=oml[:], in0=lb[:], scalar1=-1.0, scalar2=1.0, op0=ALU.mult, op1=ALU.add), deps=[t])

    rd = {k: [] for k in ("PA", "PB", "TR0", "TR1", "Ob", "SC", "KV0", "KV1", "hT0", "hT1", "w", "qs", "sg", "fg", "lf", "kk",
                          "sgg", "cum", "e1", "e2", "e3", "e4", "osq", "ms", "t1", "negmid", "el", "qt", "kt", "qh", "kh", "vT",
                          "khT", "vtm", "scm0", "scm1", "S0", "S1", "Sbf0", "Sbf1", "out")}
    out_tokens = []
    hcount = 0
    nstate = 0
    for h in range(HPC):
        t_w = None
        for which in range(4):
            src = w4[which, :, h * P:(h + 1) * P].rearrange("(kc p) c -> p kc c", p=P)
            t_w = pg.dma("pool", "hgw", (lambda e, which=which, src=src: e.dma_start(out=wsb[:, which, :, :], in_=src)), deps=rd["w"])
        rd["w"] = []
        s0 = nstate % 2
        t_s0 = pg.op("dve", (lambda e, s0=s0: e.memset(Sst[s0][:], 0.0)), deps=rd[f"S{s0}"])
        t_sb0 = pg.op("dve", (lambda e, s0=s0: e.memset(S_bf[s0][:], 0.0)), deps=rd[f"Sbf{s0}"])
        S_ready = t_s0
        Sbf_ready = t_sb0
        rd[f"S{s0}"] = []
        rd[f"Sbf{s0}"] = []
        def load_h(tb_):
            hk_ = (hbase + tb_) % 2
            t_ = None
            nq = 4 if KC >= 4 else 1
            kq = KC // nq
            for q4 in range(nq):
                src = hb[q4 * kq * P:(q4 + 1) * kq * P, tb_ * TT:(tb_ + 1) * TT].rearrange("(kc p) t -> p kc t", p=P)
                t_ = pg.dma("sp", f"hgh{hk_}", (lambda e, hk_=hk_, src=src, q4=q4, kq=kq: e.dma_start(out=hTb[hk_][:, q4 * kq:(q4 + 1) * kq, :], in_=src)),
                            deps=rd[f"hT{hk_}"])
            rd[f"hT{hk_}"] = []
            return hk_, t_

        def proj_thunks(which, bank, bname, hk_, t_h_):
            res = []
            for kc in range(KC):
                def th(kc=kc):
                    deps = ([t_w, t_h_] + rd[bname]) if kc == 0 else []
                    tk = pg.op("pe", (lambda e, kc=kc: e.matmul(bank[:], lhsT=wsb[:, which, kc, :], rhs=hTb[hk_][:, kc, :],
                                                               start=(kc == 0), stop=(kc == KC - 1))),
                               deps=deps, signal=(kc == KC - 1))
                    if kc == KC - 1:
                        rd[bname] = []
                    return tk
                res.append(th)
            return res

        hbase = hcount
        hcount += NTB
        cur_h = load_h(0)
        pend = proj_thunks(0, PA, "PA", cur_h[0], cur_h[1]) + proj_thunks(1, PB, "PB", cur_h[0], cur_h[1])
        r1 = [th() for th in pend]
        m_q, m_f = r1[KC - 1], r1[2 * KC - 1]
        for tb in range(NTB):
            hk, t_h = cur_h
            nxt_h = load_h(tb + 1) if tb + 1 < NTB else None

            def proj(which, bank, bname):
                last = None
                for th in proj_thunks(which, bank, bname, hk, t_h):
                    last = th()
                return last
            a_q = pg.op("act", lambda e: e.activation(out=qs[:], in_=PA[:], func=AF.Silu), deps=[m_q] + rd["qs"])
            rd["qs"] = []
            rd["PA"] = [a_q]
            a_sg = pg.op("act", lambda e: e.activation(out=sg[:], in_=PB[:], func=AF.Sigmoid), deps=[m_f] + rd["sg"])
            rd["sg"] = []
            rd["PB"] = [a_sg]
            d_fg = pg.op("dve", (lambda e, h=h: e.tensor_scalar(out=fg[:], in0=sg[:], scalar1=oml[:, h:h + 1], scalar2=lb[:, h:h + 1],
                                                                op0=ALU.mult, op1=ALU.add)), deps=[a_sg, t_lb] + rd["fg"])
            rd["fg"] = []
            rd["sg"] = [d_fg]
            a_lf = pg.op("act", lambda e: e.activation(out=lf[:], in_=fg[:], func=AF.Ln), deps=[d_fg] + rd["lf"])
            rd["lf"] = []
            d_kk = pg.op("dve", lambda e: e.tensor_scalar(out=kk[:], in0=fg[:], scalar1=-1.0, scalar2=1.0, op0=ALU.mult, op1=ALU.add),
                         deps=[d_fg] + rd["kk"])
            rd["kk"] = []
            rd["fg"] = [a_lf, d_kk]
            m_i = proj(2, PA, "PA")
            m_g = proj(3, PB, "PB")
            rd[f"hT{hk}"] = [m_g]
            rd["w"] = [m_g]
            a_v = pg.op("act", lambda e: e.activation(out=vT_bf[:], in_=PA[:], func=AF.Copy), deps=[m_i] + rd["vT"])
            rd["vT"] = []
            rd["PA"] = [a_v]
            a_g = pg.op("act", lambda e: e.activation(out=sgg[:], in_=PB[:], func=AF.Silu), deps=[m_g] + rd["sgg"])
            rd["sgg"] = []
            rd["PB"] = [a_g]
            d_cum = None
            for c in range(NCH):
                d_cum = pg.op("dve", (lambda e, c=c: e.tensor_tensor_scan(out=cum[:, c * C:(c + 1) * C], data0=ones64[:, :], data1=lf[:, c * C:(c + 1) * C],
                                                                         initial=0.0, op0=ALU.mult, op1=ALU.add)),
                              deps=([a_lf, t_o2] + rd["cum"]) if c == 0 else [])
            rd["cum"] = []
            rd["lf"] = [d_cum]
            d_nm = pg.op("dve", lambda e: e.tensor_scalar(out=negmid[:], in0=cum3[:, :, C // 2 - 1], scalar1=-1.0, scalar2=None, op0=ALU.mult),
                         deps=[d_cum] + rd["negmid"])
            rd["negmid"] = []
            a_el = pg.op("act", lambda e: e.activation(out=el[:], in_=cum3[:, :, C - 1], func=AF.Exp), deps=[d_cum] + rd["el"])
            rd["el"] = []
            a_e3 = pg.op("act", lambda e: e.activation(out=e3[:], in_=cum[:], func=AF.Exp), deps=[d_cum] + rd["e3"])
            rd["e3"] = []
            a_e1 = a_e2 = a_e4 = None
            for c in range(NCH):
                sl = slice(c * C, (c + 1) * C)
                a_e1 = pg.op("act", (lambda e, c=c, sl=sl: e.activation(out=e1[:, sl], in_=cum[:, sl], func=AF.Exp, bias=negmid[:, c:c + 1], scale=1.0)),
                             deps=([d_nm] + rd["e1"]) if c == 0 else [])
            for c in range(NCH):
                sl = slice(c * C, (c + 1) * C)
                a_e2 = pg.op("act", (lambda e, c=c, sl=sl: e.activation(out=e2[:, sl], in_=cum[:, sl], func=AF.Exp, bias=cum3[:, c, C // 2 - 1:C // 2], scale=-1.0)),
                             deps=([d_cum] + rd["e2"]) if c == 0 else [])
            for c in range(NCH):
                sl = slice(c * C, (c + 1) * C)
                a_e4 = pg.op("act", (lambda e, c=c, sl=sl: e.activation(out=e4[:, sl], in_=cum[:, sl], func=AF.Exp, bias=cum3[:, c, C - 1:C], scale=-1.0)),
                             deps=([d_cum] + rd["e4"]) if c == 0 else [])
            rd["e1"] = []; rd["e2"] = []; rd["e4"] = []
            rd["negmid"] = [a_e1]
            rd["cum"] = [a_e4, a_e3, a_el, d_nm]
            d_qt = pg.op("dve", lambda e: e.tensor_tensor(out=qt_bf[:], in0=qs[:], in1=e1[:], op=ALU.mult), deps=[a_q, a_e1] + rd["qt"])
            d_kt = pg.op("dve", lambda e: e.tensor_tensor(out=kt_bf[:], in0=kk[:], in1=e2[:], op=ALU.mult), deps=[d_kk, a_e2] + rd["kt"])
            d_qh = pg.op("dve", lambda e: e.tensor_tensor(out=qh_bf[:], in0=qs[:], in1=e3[:], op=ALU.mult), deps=[a_q, a_e3] + rd["qh"])
            d_kh = pg.op("dve", lambda e: e.tensor_tensor(out=kh_bf[:], in0=kk[:], in1=e4[:], op=ALU.mult), deps=[d_kk, a_e4] + rd["kh"])
            rd["qt"] = []; rd["kt"] = []; rd["qh"] = []; rd["kh"] = []
            rd["qs"] = [d_qt, d_qh]
            rd["kk"] = [d_kt, d_kh]
            rd["e1"] = [d_qt]; rd["e2"] = [d_kt]; rd["e3"] = [d_qh]; rd["e4"] = [d_kh]
            p_t0 = p_t1 = None
            for c in range(NCH):
                p_t0 = pg.op("pe", (lambda e, c=c: e.transpose(TR[0][0:C, c, :], kh_bf[:, c * C:(c + 1) * C], identb[:])),
                             deps=([d_kh, t_c1] + rd["TR0"]) if c == 0 else [], signal=(c == NCH - 1))
            for c in range(NCH):
                p_t1 = pg.op("pe", (lambda e, c=c: e.transpose(TR[1][0:C, c, :], vT_bf[:, c * C:(c + 1) * C], identb[:])),
                             deps=([a_v, t_c1] + rd["TR1"]) if c == 0 else [], signal=(c == NCH - 1))
            rd["kh"] = [p_t0]
            rd["vT"] = [p_t1]
            d_c0 = pg.op("dve", lambda e: e.tensor_copy(out=khT_sb[0:C, :, :], in_=TR[0][0:C, :, :]), deps=[p_t0, t_z] + rd["khT"])
            a_c1 = pg.op("act", lambda e: e.activation(out=vtm_sb[0:C, :, :], in_=TR[1][0:C, :, :], func=AF.Copy), deps=[p_t1, t_z] + rd["vtm"])
            rd["TR0"] = [d_c0]
            rd["TR1"] = [a_c1]
            rd["khT"] = []; rd["vtm"] = []
            p_o = None
            if nxt_h is not None:
                pend = proj_thunks(0, PA, "PA", nxt_h[0], nxt_h[1]) + proj_thunks(1, PB, "PB", nxt_h[0], nxt_h[1])
            else:
                pend = []
            pend_tok = []
            per = -(-len(pend) // NCH) if pend else 0
            for c in range(NCH):
                sl = slice(c * C, (c + 1) * C)
                ck = c % 2
                p_sc = pg.op("pe", (lambda e, sl=sl: e.matmul(SC[0:C, 0:C], lhsT=kt_bf[:, sl], rhs=qt_bf[:, sl], start=True, stop=True)),
                             deps=[d_kt, d_qt] + rd["SC"])
                d_sc = pg.op("dve", (lambda e, ck=ck: e.tensor_tensor(out=scm[ck][0:C, :], in0=SC[0:C, 0:C], in1=maskT[0:C, :], op=ALU.mult)),
                             deps=[p_sc, t_c2, t_z] + rd[f"scm{ck}"])
                rd["SC"] = [d_sc]
                sk = nstate % 2
                p_o1 = pg.op("pe", (lambda e, c=c, sl=sl, ck=ck: e.matmul(Ob[:, sl], lhsT=vtm_sb[:, c, :], rhs=scm[ck][:, :], start=True, stop=False)),
                             deps=[d_sc, a_c1] + (rd["Ob"] if c == 0 else []), signal=False)
                p_o = pg.op("pe", (lambda e, sl=sl, sk=sk: e.matmul(Ob[:, sl], lhsT=S_bf[sk][:, :], rhs=qh_bf[:, sl], start=False, stop=True)),
                            deps=[Sbf_ready, d_qh])
                rd[f"scm{ck}"] = [p_o]
                rd[f"Sbf{sk}"] = [p_o]
                kvk = c % 2
                p_kv = pg.op("pe", (lambda e, c=c, kvk=kvk: e.matmul(KV[kvk][:, 0:P], lhsT=khT_sb[:, c, :], rhs=vtm_sb[:, c, :], start=True, stop=True)),
                             deps=[d_c0, a_c1] + rd[f"KV{kvk}"])
                sn = (nstate + 1) % 2
                d_S = pg.op("dve", (lambda e, c=c, sk=sk, sn=sn, kvk=kvk: e.scalar_tensor_tensor(
                    out=Sst[sn][:], in0=Sst[sk][:], scalar=el[:, c:c + 1], in1=KV[kvk][:, 0:P], op0=ALU.mult, op1=ALU.add)),
                    deps=[p_kv, a_el, S_ready] + rd[f"S{sn}"])
                rd[f"KV{kvk}"] = [d_S]
                rd[f"S{sn}"] = []
                a_Sb = pg.op("act", (lambda e, sn=sn: e.activation(out=S_bf[sn][:], in_=Sst[sn][:], func=AF.Copy)), deps=[d_S] + rd[f"Sbf{sn}"])
                rd[f"Sbf{sn}"] = []
                rd[f"S{sk}"] = [d_S]
                rd[f"S{sn}"] = [a_Sb]
                S_ready = d_S
                Sbf_ready = a_Sb
                nstate += 1
                for th in pend[c * per:(c + 1) * per]:
                    pend_tok.append(th())
            if pend:
                m_q, m_f = pend_tok[KC - 1], pend_tok[2 * KC - 1]
                cur_h = nxt_h
            rd["khT"] = [p_kv]
            rd["vtm"] = [p_kv, p_o]
            rd["kt"] = [p_sc]; rd["qt"] = [p_sc]; rd["qh"] = [p_o]
            rd["el"] = [d_S]
            a_sq = pg.op("act", lambda e: e.activation(out=osq[:], in_=Ob[:], func=AF.Square), deps=[p_o] + rd["osq"])
            p_ss = pg.op("pe", lambda e: e.matmul(SC[:], lhsT=onesf[:], rhs=osq[:], start=True, stop=True), deps=[a_sq, t_o1] + rd["SC"])
            rd["osq"] = [p_ss]
            d_ms = pg.op("dve", lambda e: e.tensor_scalar(out=ms[:], in0=SC[:], scalar1=1.0 / P, scalar2=float(RMS_EPS), op0=ALU.mult, op1=ALU.add),
                         deps=[p_ss] + rd["ms"])
            rd["SC"] = [d_ms]
            a_sd = pg.op("act", lambda e: e.activation(out=ms[:], in_=ms[:], func=AF.Sqrt), deps=[d_ms])
            d_ri = pg.op("dve", lambda e: e.reciprocal(out=ms[:], in_=ms[:]), deps=[a_sd])
            d_t1 = pg.op("dve", lambda e: e.tensor_tensor(out=t1[:], in0=Ob[:], in1=ms[:], op=ALU.mult), deps=[d_ri, p_o] + rd["t1"])
            rd["Ob"] = [d_t1, a_sq]
            d_ob = pg.op("dve", (lambda e, h=h, tb=tb: e.scalar_tensor_tensor(out=out_sb[:, tb * TT:(tb + 1) * TT], in0=t1[:], scalar=hgp_sb[:, h, 3:4], in1=sgg[:],
                                                                              op0=ALU.mult, op1=ALU.mult)), deps=[d_t1, a_g] + (rd["out"] if tb == 0 else []))
            rd["ms"] = [d_t1]
            rd["t1"] = [d_ob]
            rd["sgg"] = [d_ob]
        t_o = pg.dma("sp", "hgo", (lambda e, h=h: e.dma_start(out=oT[h * P:(h + 1) * P, :], in_=out_sb[:, :])), deps=[d_ob])
        rd["out"] = [t_o]
        out_tokens.append(t_o)
    return out_tokens


import contextlib

NCORES = 8
TPC = SEQ // NCORES
HPC = 4
KC_ = D_MODEL // P


def _ln_lay(v):
    return np.ascontiguousarray(np.asarray(v, np.float32).reshape(-1, P).T)


def build_token_launch(stages):
    nc = bass.Bass("TRN2", target_bir_lowering=False)
    D, T, F = D_MODEL, TPC, D_FF
    dt = lambda name, shape, dty, kind: nc.dram_tensor(name, shape, dty, kind=kind).ap()
    xT = dt("xT", [D, T], F32, "ExternalInput")
    rscr = dt("rscr", [D, T], F32, "Internal")
    pg = Prog(nc)
    with contextlib.ExitStack() as es:
        st = ProjLN3(pg, nc, es, D) if len(stages) == 1 else ProjLN(pg, nc, es, D, F // P, True)
        cur = xT
        toks = ()
        for si, sg in enumerate(stages):
            lastst = si == len(stages) - 1
            nm = sg["name"]
            gT = dt(nm + "_g", [P, D // P], F32, "ExternalInput")
            bT = dt(nm + "_b", [P, D // P], F32, "ExternalInput")
            if lastst:
                yT = dt("yT", [D, T], F32, "ExternalOutput")
                ybT = dt("ybT", [D, T], BF16, "ExternalOutput") if sg.get("want_b") else None
            else:
                yT = dt(f"y_int{si}", [D, T], F32, "Internal")
                ybT = None
            if sg["kind"] == "ffn":
                w_in = dt(nm + "_win", [D, 2 * F], F32, "ExternalInput")
                w_out = dt(nm + "_wout", [F, D], F32, "ExternalInput")
                toks = st.run(T=T, K2=F, xT=cur, w_out=w_out, gT=gT, bT=bT, yT=yT, ybT=ybT, rscr=rscr, scale=0.5,
                              F=F, w_in=w_in, in_deps=toks)
            else:
                oin = dt("oin", [D, T], BF16, "ExternalInput")
                w_out = dt(nm + "_wout", [D, D], F32, "ExternalInput")
                toks = st.run(T=T, K2=D, xT=cur, w_out=w_out, gT=gT, bT=bT, yT=yT, ybT=ybT, rscr=rscr, scale=1.0,
                              inT=oin, in_deps=toks)
            cur = yT
        pg.emit(toks)
    return nc


def build_hgrn_launch():
    nc = bass.Bass("TRN2", target_bir_lowering=False)
    D, S = D_MODEL, SEQ
    hb = nc.dram_tensor("hb", [D, S], BF16, kind="ExternalInput").ap()
    w4 = nc.dram_tensor("w4", [4, D, HPC * P], F32, kind="ExternalInput").ap()
    hgp = nc.dram_tensor("hgp", [P, HPC * 4], F32, kind="ExternalInput").ap()
    consts = nc.dram_tensor("consts", [P, P + 64], F32, kind="ExternalInput").ap()
    oT = nc.dram_tensor("oT", [HPC * P, S], BF16, kind="ExternalOutput").ap()
    pg = Prog(nc)
    with contextlib.ExitStack() as es:
        toks = emit_hgrn(pg, nc, es, D=D, S=S, HPC=HPC, hb=hb, w4=w4, hgp=hgp, consts=consts, oT=oT)
        pg.emit(toks)
    return nc


def build_moba_launch():
    nc = bass.Bass("TRN2", target_bir_lowering=False)
    D, S = D_MODEL, SEQ
    NBLK = S // 256
    hb = nc.dram_tensor("hb", [D, S], BF16, kind="ExternalInput").ap()
    wqkv = nc.dram_tensor("wqkv", [3, D, HPC * P], F32, kind="ExternalInput").ap()
    consts = nc.dram_tensor("consts", [P, 2 * P + NBLK * NBLK], F32, kind="ExternalInput").ap()
    oT = nc.dram_tensor("oT", [HPC * P, S], BF16, kind="ExternalOutput").ap()
    pg = Prog(nc)
    with contextlib.ExitStack() as es:
        toks = emit_moba(pg, nc, es, D=D, S=S, HPC=HPC, hb=hb, wqkv=wqkv, consts=consts, oT=oT)
        pg.emit(toks)
    return nc


def hg_consts():
    c = np.zeros((P, P + 64), np.float32)
    c[:, :P] = np.eye(P)
    s = np.arange(P)[:, None]
    t = np.arange(64)[None, :]
    c[:, P:] = ((s <= t) & (s < 64)).astype(np.float32)
    return c


def moba_consts(NBLK):
    c = np.zeros((P, 2 * P + NBLK * NBLK), np.float32)
    c[:, :P] = np.eye(P)
    q = np.arange(P)[:, None]
    k = np.arange(P)[None, :]
    c[:, P:2 * P] = np.where(k <= q, 0.0, NEG)
    j = np.arange(NBLK)[:, None]
    n = np.arange(NBLK)[None, :]
    c[:, 2 * P:] = np.where(n < j, 0.0, -1e30).reshape(1, -1)
    return c


def _run(nc, in_maps):
    res = run_bass_kernel_spmd(nc, in_maps, core_ids=list(range(NCORES)))
    return res.results


def _head_cols(w, nsec, c):
    D = D_MODEL
    return np.ascontiguousarray(np.stack([w[:, s * D + c * HPC * P: s * D + (c + 1) * HPC * P] for s in range(nsec)], 0))


def _tok_slices(full_T):
    return [np.ascontiguousarray(full_T[:, c * TPC:(c + 1) * TPC]) for c in range(NCORES)]


def kernel(**inp):
    f32 = lambda a: np.asarray(a, np.float32)
    x = f32(inp["x"])[0]
    xT = np.ascontiguousarray(x.T)
    nc1 = build_token_launch([dict(kind="ffn", name="f", want_b=True)])
    w_in, w_out = f32(inp["l0_ffn1_in"]), f32(inp["l0_ffn1_out"])
    g, b = _ln_lay(inp["l0_ln1_g"]), _ln_lay(inp["l0_ln1_b"])
    xs = _tok_slices(xT)
    r1 = _run(nc1, [{"xT": xs[c], "f_win": w_in, "f_wout": w_out, "f_g": g, "f_b": b} for c in range(NCORES)])
    del w_in, w_out
    y1 = [r["yT"] for r in r1]
    hb = np.ascontiguousarray(np.concatenate([r["ybT"] for r in r1], axis=1))
    nc2 = build_hgrn_launch()
    whg = f32(inp["l0_hg_in"])
    lbl = f32(inp["lb_logits"])
    ng = f32(inp["l0_hg_norm_g"])
    hc = hg_consts()
    maps = []
    for c in range(NCORES):
        hgp = np.zeros((P, HPC, 4), np.float32)
        for h in range(HPC):
            ch = (c * HPC + h) * P
            hgp[:, h, 0:3] = lbl[:, ch:ch + P].T
            hgp[:, h, 3] = ng[ch:ch + P]
        maps.append({"hb": hb, "w4": _head_cols(whg, 4, c), "hgp": hgp.reshape(P, -1), "consts": hc})
    r2 = _run(nc2, maps)
    del whg, maps
    oT = np.concatenate([r["oT"] for r in r2], axis=0)
    nc3 = build_token_launch([dict(kind="proj", name="p"), dict(kind="ffn", name="f"), dict(kind="ffn", name="h", want_b=True)])
    os_ = _tok_slices(oT)
    base = {"p_wout": f32(inp["l0_hg_out"]), "p_g": _ln_lay(inp["l0_ln2_g"]), "p_b": _ln_lay(inp["l0_ln2_b"]),
            "f_win": f32(inp["l0_ffn2_in"]), "f_wout": f32(inp["l0_ffn2_out"]), "f_g": _ln_lay(inp["l0_ln3_g"]), "f_b": _ln_lay(inp["l0_ln3_b"]),
            "h_win": f32(inp["l1_ffn1_in"]), "h_wout": f32(inp["l1_ffn1_out"]), "h_g": _ln_lay(inp["l1_ln1_g"]), "h_b": _ln_lay(inp["l1_ln1_b"])}
    r3 = _run(nc3, [dict(base, xT=y1[c], oin=os_[c]) for c in range(NCORES)])
    del base
    y4 = [r["yT"] for r in r3]
    hb = np.ascontiguousarray(np.concatenate([r["ybT"] for r in r3], axis=1))
    nc4 = build_moba_launch()
    wmb = f32(inp["l1_mb_in"])
    mc = moba_consts(SEQ // 256)
    r4 = _run(nc4, [{"hb": hb, "wqkv": _head_cols(wmb, 3, c), "consts": mc} for c in range(NCORES)])
    del wmb
    oT = np.concatenate([r["oT"] for r in r4], axis=0)
    nc5 = build_token_launch([dict(kind="proj", name="p"), dict(kind="ffn", name="f")])
    os_ = _tok_slices(oT)
    base = {"p_wout": f32(inp["l1_mb_out"]), "p_g": _ln_lay(inp["l1_ln2_g"]), "p_b": _ln_lay(inp["l1_ln2_b"]),
            "f_win": f32(inp["l1_ffn2_in"]), "f_wout": f32(inp["l1_ffn2_out"]), "f_g": _ln_lay(inp["l1_ln3_g"]), "f_b": _ln_lay(inp["l1_ln3_b"])}
    r5 = _run(nc5, [dict(base, xT=y4[c], oin=os_[c]) for c in range(NCORES)])
    yT = np.concatenate([r["yT"] for r in r5], axis=1)
    return np.ascontiguousarray(yT.T)[None].astype(np.float32)
```

```python
import numpy as np
import concourse.bass as bass
import concourse.mybir as mybir
from concourse.bass_utils import run_bass_kernel_spmd

F32 = mybir.dt.float32
BF16 = mybir.dt.bfloat16
AF = mybir.ActivationFunctionType
ALU = mybir.AluOpType
AX = mybir.AxisListType

D_MODEL = 4096
SEQ = 8192
D_FF = 11008
DEPTH = 2
ALPHA = float((2 * DEPTH) ** 0.25)
LN_EPS = 1e-5
RMS_EPS = 1e-6
P = 128
TT = 512
DBG_SKIP_W = False
DBG_SKIP_MM = False


class Prog:
    ENGS = ("pe", "act", "dve", "pool", "sp")

    def __init__(self, nc):
        self.nc = nc
        self.ops = {e: [] for e in self.ENGS}
        self.cnt = {e: 0 for e in self.ENGS}
        self.chan_cnt = {}
        self.chan_order = []

    def op(self, eng, fn, deps=(), signal=True):
        tok = None
        if signal:
            self.cnt[eng] += 1
            tok = (eng, self.cnt[eng])
        self.ops[eng].append((fn, [d for d in deps if d is not None], tok))
        return tok

    def dma(self, eng, chan, fn, deps=()):
        if chan not in self.chan_cnt:
            self.chan_cnt[chan] = 0
            self.chan_order.append(chan)
        self.chan_cnt[chan] += 16
        tok = (chan, self.chan_cnt[chan])
        self.ops[eng].append((fn, [d for d in deps if d is not None], tok))
        return tok

    def emit(self, final_waits):
        nc = self.nc
        names = list(self.ENGS) + self.chan_order
        sems = {}
        import contextlib
        with contextlib.ExitStack() as st:
            for n in names:
                sems[n] = st.enter_context(nc.semaphore("s_" + n))
            block = st.enter_context(nc.Block())
            engmap = {"pe": block.tensor, "act": block.scalar, "dve": block.vector,
                      "pool": block.gpsimd, "sp": block.sync}

            def make(ename):
                ops = self.ops[ename]

                def body(eng):
                    waited = {}
                    for fn, deps, tok in ops:
                        for (src, val) in deps:
                            if waited.get(src, 0) < val:
                                eng.wait_ge(sems[src], val)
                                waited[src] = val
                        ins = fn(eng)
                        if tok is not None:
                            ins.then_inc(sems[tok[0]], 16 if tok[0] not in self.ENGS else 1)
                    if ename == "sp":
                        for (src, val) in final_waits:
                            if val > 0:
                                eng.wait_ge(sems[src], val)
                return body

            for ename in self.ENGS:
                engmap[ename](make(ename))


class Ring:
    def __init__(self, name, tiles):
        self.tiles = tiles
        self.n = len(tiles)
        self.i = 0
        self.readers = [None] * self.n
        self.name = name

    def next(self):
        k = self.i % self.n
        self.i += 1
        return k

    def chan(self, k):
        return f"{self.name}{k}"


class ProjLN:
    def __init__(self, pg, nc, es, D, FCmax, need_h, tag=""):
        self.pg, self.nc, self.D = pg, nc, D
        KC = D // P
        sb = lambda name, shape, dt: es.enter_context(nc.sbuf_tensor(name + tag, shape, dt))
        ps = lambda name, shape, dt: es.enter_context(nc.psum_tensor(name + tag, shape, dt))
        self.tag = tag
        self.hT = sb("hT", [P, KC, TT], BF16) if need_h else None
        self.actT = sb("actT", [P, FCmax, TT], BF16)
        self.wslots = [sb(f"ws{k}", [P, 32, P], BF16) for k in range(6)]
        self.wring = Ring("w" + tag, self.wslots)
        self.ones = sb("ones", [P, P], F32)
        self.gsb = sb("gsb", [P, KC], F32)
        self.bsb = sb("bsb", [P, KC], F32)
        self.silu_t = [sb(f"silu{k}", [P, TT], F32) for k in range(2)]
        self.xs = [sb(f"xs{k}", [P, TT], F32) for k in range(2)]
        self.rb = [sb(f"rb{k}", [P, TT], F32) for k in range(2)]
        self.sq = [sb(f"sq{k}", [P, TT], F32) for k in range(2)]
        self.yo = [sb(f"yo{k}", [P, TT], F32) for k in range(2)]
        self.yob = [sb(f"yob{k}", [P, TT], BF16) for k in range(2)]
        self.mean = sb("mean", [P, TT], F32)
        self.msq = sb("msq", [P, TT], F32)
        self.rstd = sb("rstd", [P, TT], F32)
        self.nmr = sb("nmr", [P, TT], F32)
        self.pa = [ps(f"pa{k}", [P, TT], F32) for k in range(4)]
        self.pb = [ps(f"pb{k}", [P, TT], F32) for k in range(2)]
        self.pst = [ps(f"pst{k}", [P, TT], F32) for k in range(2)]
        self.t_ones = pg.op("dve", lambda e: e.memset(self.ones[:], 1.0))
        self.pa_rd = [[] for _ in range(4)]
        self.pb_rd = [[] for _ in range(2)]
        self.silu_rd = [[] for _ in range(2)]
        self.xs_rd = [[] for _ in range(2)]
        self.rb_rd = [[] for _ in range(2)]
        self.sq_rd = [[] for _ in range(2)]
        self.yo_rd = [[] for _ in range(2)]
        self.yob_rd = [[] for _ in range(2)]
        self.act_rd = []
        self.hT_rd = []
        self.stat_rd = []
        self.pst_rd = []
        self.gb_rd = []

    def run(self, *, T, K2, xT, w_out, gT, bT, yT, ybT, rscr, scale, F=None, w_in=None, inT=None, in_deps=()):
        pg, D = self.pg, self.D
        tag = self.tag
        KC = D // P
        FC = K2 // P
        NT = T // TT
        hT, actT, wslots, wring, ones = self.hT, self.actT, self.wslots, self.wring, self.ones
        gsb, bsb, silu_t, xs, rb, sq, yo, yob = self.gsb, self.bsb, self.silu_t, self.xs, self.rb, self.sq, self.yo, self.yob
        mean, msq, rstd, nmr, pa, pb, pst = self.mean, self.msq, self.rstd, self.nmr, self.pa, self.pb, self.pst
        if in_deps:
            pg.op("sp", lambda e: e.nop(), deps=list(in_deps))
            pg.op("pool", lambda e: e.nop(), deps=list(in_deps))
        t_g = pg.dma("sp", "gb" + tag, lambda e: e.dma_start(out=gsb[:, 0:KC], in_=gT[:, :]), deps=self.gb_rd)
        t_b = pg.dma("sp", "gb" + tag, lambda e: e.dma_start(out=bsb[:, 0:KC], in_=bT[:, :]), deps=self.gb_rd)
        t_g = t_b
        out_tokens = []
        for tt in range(NT):
            c0 = tt * TT
            if F is not None:
                t_h = None
                nq = 4 if KC >= 4 else 1
                kq = KC // nq
                for q4 in range(nq):
                    src = xT[q4 * kq * P:(q4 + 1) * kq * P, c0:c0 + TT].rearrange("(kc p) t -> p kc t", p=P)
                    t_h = pg.dma("pool", "hT" + tag,
                                 (lambda e, src=src, q4=q4, kq=kq: e.dma_start(out=hT[:, q4 * kq:(q4 + 1) * kq, :], in_=src)),
                                 deps=self.hT_rd)
                self.hT_rd = []
                for j in range(FC):
                    toks = []
                    for which in range(2):
                        k = wring.next()
                        col = which * F + j * P
                        src = w_in[:, col:col + P].rearrange("(kc p) c -> p kc c", p=P)
                        tl = pg.dma("pool", wring.chan(k),
                                    (lambda e, k=k, src=src: e.dma_start(out=wslots[k][:, 0:(1 if DBG_SKIP_W else KC), :], in_=src[:, 0:(1 if DBG_SKIP_W else KC), :])),
                                    deps=[wring.readers[k]])
                        toks.append([k, tl, None])
                    pi = (j % 2) * 2
                    for which in range(2):
                        k, tl, _ = toks[which]
                        bank = pa[pi + which]
                        last = None
                        for kc in range(KC):
                            deps = [tl, t_h] + self.pa_rd[pi + which] if kc == 0 else []
                            last = pg.op("pe", (lambda e, bank=bank, k=k, kc=kc: e.matmul(
                                bank[:], lhsT=wslots[k][:, kc, :], rhs=hT[:, kc, :], start=(kc == 0), stop=(kc == KC - 1))),
                                deps=deps, signal=(kc == KC - 1))
                        wring.readers[k] = last
                        toks[which][2] = last
                    self.hT_rd = [toks[1][2]]
                    sk = j % 2
                    t_s = pg.op("act", (lambda e, sk=sk, bank=pa[pi]: e.activation(out=silu_t[sk][:], in_=bank[:], func=AF.Silu)),
                                deps=[toks[0][2]] + self.silu_rd[sk])
                    t_m = pg.op("dve", (lambda e, sk=sk, j=j, bank=pa[pi + 1]: e.tensor_tensor(
                        out=actT[:, j, :], in0=silu_t[sk][:], in1=bank[:], op=ALU.mult)),
                        deps=[t_s, toks[1][2]] + self.act_rd)
                    self.silu_rd[sk] = [t_m]
                    self.pa_rd[pi] = [t_s]
                    self.pa_rd[pi + 1] = [t_m]
                    act_ready = t_m
                self.act_rd = []
            else:
                act_ready = None
                nq = 4
                fq = (FC + nq - 1) // nq
                for q4 in range(nq):
                    f0, f1 = q4 * fq, min(FC, (q4 + 1) * fq)
                    if f0 >= f1:
                        continue
                    src = inT[f0 * P:f1 * P, c0:c0 + TT].rearrange("(kc p) t -> p kc t", p=P)
                    act_ready = pg.dma("sp", "actin" + tag,
                                       (lambda e, src=src, f0=f0, f1=f1: e.dma_start(out=actT[:, f0:f1, :], in_=src)),
                                       deps=self.act_rd)
                self.act_rd = []
            rs_tok = [None] * KC
            last_stat = None
            for i in range(KC):
                xk = i % 2
                srcx = xT[i * P:(i + 1) * P, c0:c0 + TT]
                t_x = pg.dma("sp", f"xs{xk}" + tag, (lambda e, xk=xk, srcx=srcx: e.dma_start(out=xs[xk][:], in_=srcx)),
                             deps=self.xs_rd[xk])
                nseg = (FC + 31) // 32
                segs = []
                for s in range(nseg):
                    f0, f1 = s * 32, min(FC, (s + 1) * 32)
                    k = wring.next()
                    src = w_out[f0 * P:f1 * P, i * P:(i + 1) * P].rearrange("(fc p) c -> p fc c", p=P)
                    tl = pg.dma("pool", wring.chan(k),
                                (lambda e, k=k, src=src, n=f1 - f0: e.dma_start(out=wslots[k][:, 0:(1 if DBG_SKIP_W else n), :], in_=src[:, 0:(1 if DBG_SKIP_W else n), :])),
                                deps=[wring.readers[k]])
                    segs.append((k, tl, f0, f1))
                bk = i % 2
                last = None
                for (k, tl, f0, f1) in segs:
                    for fc in range(f0, f1):
                        deps = []
                        if fc == f0:
                            deps = [tl]
                            if fc == 0:
                                deps += [act_ready] + self.pb_rd[bk]
                        last = pg.op("pe", (lambda e, k=k, fc=fc, f0=f0, bk=bk: e.matmul(
                            pb[bk][:], lhsT=wslots[k][:, fc - f0, :], rhs=actT[:, fc, :], start=(fc == 0), stop=(fc == FC - 1))),
                            deps=deps, signal=(fc == f1 - 1))
                    wring.readers[k] = last
                self.act_rd = [last]
                t_ax = pg.op("act", (lambda e, xk=xk: e.activation(out=xs[xk][:], in_=xs[xk][:], func=AF.Copy, scale=ALPHA)),
                             deps=[t_x])
                t_r = pg.op("dve", (lambda e, xk=xk, bk=bk: e.scalar_tensor_tensor(
                    out=rb[xk][:], in0=pb[bk][:], scalar=float(scale), in1=xs[xk][:], op0=ALU.mult, op1=ALU.add)),
                    deps=[t_ax, last] + self.rb_rd[xk])
                self.pb_rd[bk] = [t_r]
                self.xs_rd[xk] = [t_r]
                t_sq = pg.op("act", (lambda e, xk=xk: e.activation(out=sq[xk][:], in_=rb[xk][:], func=AF.Square)),
                             deps=[t_r] + self.sq_rd[xk])
                t_st0 = pg.op("pe", (lambda e, xk=xk, i=i: e.matmul(pst[0][:], lhsT=ones[:], rhs=rb[xk][:], start=(i == 0), stop=(i == KC - 1))),
                              deps=[t_r, self.t_ones] + (self.pst_rd if i == 0 else []))
                t_st1 = pg.op("pe", (lambda e, xk=xk, i=i: e.matmul(pst[1][:], lhsT=ones[:], rhs=sq[xk][:], start=(i == 0), stop=(i == KC - 1))),
                              deps=[t_sq])
                self.sq_rd[xk] = [t_st1]
                dstr = rscr[i * P:(i + 1) * P, c0:c0 + TT]
                t_rs = pg.dma("sp", f"rst{xk}" + tag, (lambda e, xk=xk, dstr=dstr: e.dma_start(out=dstr, in_=rb[xk][:])),
                              deps=[t_r])
                rs_tok[i] = t_rs
                self.rb_rd[xk] = [t_sq, t_st0, t_rs]
                last_stat = t_st1
            invD = 1.0 / D
            t_mean = pg.op("act", lambda e: e.activation(out=mean[:], in_=pst[0][:], func=AF.Copy, scale=invD),
                           deps=[last_stat] + self.stat_rd)
            t_msq = pg.op("dve", lambda e: e.tensor_tensor(out=msq[:], in0=mean[:], in1=mean[:], op=ALU.mult),
                          deps=[t_mean] + self.stat_rd)
            t_var = pg.op("dve", lambda e: e.scalar_tensor_tensor(out=msq[:], in0=pst[1][:], scalar=invD, in1=msq[:],
                                                                 op0=ALU.mult, op1=ALU.subtract), deps=[t_msq, last_stat])
            self.pst_rd = [t_var]
            t_veps = pg.op("dve", lambda e: e.tensor_scalar(out=msq[:], in0=msq[:], scalar1=float(LN_EPS), scalar2=None, op0=ALU.add),
                           deps=[t_var])
            t_std = pg.op("act", lambda e: e.activation(out=rstd[:], in_=msq[:], func=AF.Sqrt), deps=[t_veps] + self.stat_rd)
            t_rstd = pg.op("dve", lambda e: e.reciprocal(out=rstd[:], in_=rstd[:]), deps=[t_std])
            t_nmr = pg.op("dve", lambda e: e.scalar_tensor_tensor(out=nmr[:], in0=mean[:], scalar=-1.0, in1=rstd[:],
                                                                 op0=ALU.mult, op1=ALU.mult), deps=[t_rstd, t_mean] + self.stat_rd)
            for i in range(KC):
                xk = i % 2
                srcr = rscr[i * P:(i + 1) * P, c0:c0 + TT]
                t_l = pg.dma("sp", f"rld{xk}" + tag, (lambda e, xk=xk, srcr=srcr: e.dma_start(out=rb[xk][:], in_=srcr)),
                             deps=[rs_tok[i]] + self.rb_rd[xk])
                t_1 = pg.op("dve", (lambda e, xk=xk: e.tensor_tensor(out=rb[xk][:], in0=rb[xk][:], in1=rstd[:], op=ALU.mult)),
                            deps=[t_l, t_rstd])
                t_2 = pg.op("dve", (lambda e, xk=xk: e.tensor_tensor(out=rb[xk][:], in0=rb[xk][:], in1=nmr[:], op=ALU.add)),
                            deps=[t_1, t_nmr])
                t_y = pg.op("act", (lambda e, xk=xk, i=i: e.activation(out=yo[xk][:], in_=rb[xk][:], func=AF.Identity,
                                                                        scale=gsb[:, i:i + 1], bias=bsb[:, i:i + 1])),
                            deps=[t_2, t_g, t_b] + self.yo_rd[xk])
                dsty = yT[i * P:(i + 1) * P, c0:c0 + TT]
                t_ys = pg.dma("sp", f"yst{xk}" + tag, (lambda e, xk=xk, dsty=dsty: e.dma_start(out=dsty, in_=yo[xk][:])), deps=[t_y])
                out_tokens.append(t_ys)
                self.yo_rd[xk] = [t_ys]
                if ybT is not None:
                    t_yb = pg.op("dve", (lambda e, xk=xk: e.tensor_copy(out=yob[xk][:], in_=yo[xk][:])), deps=[t_y] + self.yob_rd[xk])
                    dstb = ybT[i * P:(i + 1) * P, c0:c0 + TT]
                    t_ybs = pg.dma("sp", f"ybst{xk}" + tag, (lambda e, xk=xk, dstb=dstb: e.dma_start(out=dstb, in_=yob[xk][:])), deps=[t_yb])
                    self.yob_rd[xk] = [t_ybs]
                    out_tokens.append(t_ybs)
                    self.yo_rd[xk] = [t_ys, t_yb]
                self.rb_rd[xk] = [t_y]
                self.stat_rd = [t_2]
                self.gb_rd = [t_y]
        return out_tokens


class ProjLN3:
    FCP = 22
    NHMAX = 2

    def __init__(self, pg, nc, es, D, tag=""):
        self.pg, self.nc, self.D = pg, nc, D
        KC = D // P
        sb = lambda name, shape, dt: es.enter_context(nc.sbuf_tensor(name + tag, shape, dt))
        ps = lambda name, shape, dt: es.enter_context(nc.psum_tensor(name + tag, shape, dt))
        self.tag = tag
        TW = self.NHMAX * TT
        self.hT = sb("hT", [P, KC, TW], BF16)
        self.actT = sb("actT", [P, self.FCP, TW], BF16)
        self.wslots = [sb(f"ws{k}", [P, 32, P], BF16) for k in range(6)]
        self.wring = Ring("w" + tag, self.wslots)
        self.ones = sb("ones", [P, P], F32)
        self.gsb = sb("gsb", [P, KC], F32)
        self.bsb = sb("bsb", [P, KC], F32)
        self.silu_t = [sb(f"silu{k}", [P, TT], F32) for k in range(2)]
        self.xs = [sb(f"xs{k}", [P, TT], F32) for k in range(2)]
        self.rb = [sb(f"rb{k}", [P, TT], F32) for k in range(2)]
        self.sq = [sb(f"sq{k}", [P, TT], F32) for k in range(2)]
        self.yo = [sb(f"yo{k}", [P, TT], F32) for k in range(2)]
        self.yob = [sb(f"yob{k}", [P, TT], BF16) for k in range(2)]
        self.mean = sb("mean", [P, TT], F32)
        self.msq = sb("msq", [P, TT], F32)
        self.rstd = sb("rstd", [P, TT], F32)
        self.nmr = sb("nmr", [P, TT], F32)
        self.pa = [ps(f"pa{k}", [P, TT], F32) for k in range(4)]
        self.pb = [ps(f"pb{k}", [P, TT], F32) for k in range(2)]
        self.pst = [ps(f"pst{k}", [P, TT], F32) for k in range(2)]
        self.t_ones = pg.op("dve", lambda e: e.memset(self.ones[:], 1.0))
        z = lambda n: [[] for _ in range(n)]
        self.pa_rd, self.pb_rd, self.pst_rd = z(4), z(2), z(2)
        self.silu_rd, self.xs_rd, self.rb_rd, self.sq_rd, self.yo_rd, self.yob_rd = z(2), z(2), z(2), z(2), z(2), z(2)
        self.act_rd = []
        self.hT_rd = []
        self.stat_rd = []
        self.gb_rd = []
        self.h_written = [None] * self.NHMAX

    def run(self, *, T, K2, xT, w_out, gT, bT, yT, ybT, rscr, scale, F=None, w_in=None, inT=None, in_deps=(),
            h_resident=False, keep_h=False):
        pg, D, tag = self.pg, self.D, self.tag
        KC = D // P
        FC = K2 // P
        hT, actT, wslots, wring, ones = self.hT, self.actT, self.wslots, self.wring, self.ones
        gsb, bsb, silu_t, xs, rb, sq, yo, yob = self.gsb, self.bsb, self.silu_t, self.xs, self.rb, self.sq, self.yo, self.yob
        mean, msq, rstd, nmr, pa, pb, pst = self.mean, self.msq, self.rstd, self.nmr, self.pa, self.pb, self.pst
        if in_deps:
            pg.op("sp", lambda e: e.nop(), deps=list(in_deps))
            if not h_resident:
                pg.op("pool", lambda e: e.nop(), deps=list(in_deps))
        t_g = pg.dma("sp", "gb" + tag, lambda e: e.dma_start(out=gsb[:, 0:KC], in_=gT[:, :]), deps=self.gb_rd)
        t_b = pg.dma("sp", "gb" + tag, lambda e: e.dma_start(out=bsb[:, 0:KC], in_=bT[:, :]), deps=self.gb_rd)
        t_g = t_b
        out_tokens = []
        TW = self.NHMAX * TT
        for c0 in range(0, T, TW):
            W = min(TW, T - c0)
            NH = W // TT
            t_hh = [None] * NH
            if h_resident:
                assert T == TW and F is not None
                t_hh = list(self.h_written)
            else:
                for half in range(NH):
                    for q2 in range(2 if KC >= 2 else 1):
                        kq = KC // (2 if KC >= 2 else 1)
                        srcT = xT if F is not None else inT
                        src = srcT[q2 * kq * P:(q2 + 1) * kq * P, c0 + half * TT:c0 + (half + 1) * TT].rearrange("(kc p) t -> p kc t", p=P)
                        t_hh[half] = pg.dma("pool" if F is not None else "sp", f"hT{half}" + tag,
                                            (lambda e, src=src, q2=q2, kq=kq, half=half: e.dma_start(
                                                out=hT[:, q2 * kq:(q2 + 1) * kq, half * TT:(half + 1) * TT], in_=src)),
                                            deps=self.hT_rd)
            self.hT_rd = []
            if F is not None:
                parts = [(f0, min(FC, f0 + self.FCP)) for f0 in range(0, FC, self.FCP)]
            else:
                parts = [(0, FC)]
            nparts = len(parts)
            rs_tok = {}
            last_stat = [None] * NH
            for pi_, (pf0, pf1) in enumerate(parts):
                lastp = pi_ == nparts - 1
                if F is not None:
                    for j in range(pf0, pf1):
                        toks = []
                        for which in range(2):
                            k = wring.next()
                            col = which * F + j * P
                            src = w_in[:, col:col + P].rearrange("(kc p) c -> p kc c", p=P)
                            tl = pg.dma("pool", wring.chan(k),
                                        (lambda e, k=k, src=src: e.dma_start(out=wslots[k][:, 0:(1 if DBG_SKIP_W else KC), :], in_=src[:, 0:(1 if DBG_SKIP_W else KC), :])),
                                        deps=[wring.readers[k]])
                            toks.append((k, tl))
                        for half in range(NH):
                            hs = slice(half * TT, (half + 1) * TT)
                            mm = []
                            for which in range(2):
                                k, tl = toks[which]
                                bi = half * 2 + which
                                bank = pa[bi]
                                last = None
                                for kc in range(KC):
                                    deps = ([tl, t_hh[half]] + self.pa_rd[bi]) if kc == 0 else []
                                    last = pg.op("pe", (lambda e, bank=bank, k=k, kc=kc, hs=hs: e.matmul(
                                        bank[:], lhsT=wslots[k][:, kc, :], rhs=hT[:, kc, hs], start=(kc == 0), stop=(kc == KC - 1))),
                                        deps=deps, signal=(kc == KC - 1))
                                self.pa_rd[bi] = []
                                wring.readers[k] = last
                                mm.append(last)
                            self.hT_rd = [mm[1]]
                            sk = half
                            t_s = pg.op("act", (lambda e, sk=sk, bank=pa[half * 2]: e.activation(out=silu_t[sk][:], in_=bank[:], func=AF.Silu)),
                                        deps=[mm[0]] + self.silu_rd[sk])
                            t_m = pg.op("dve", (lambda e, sk=sk, jj=j - pf0, hs=hs, bank=pa[half * 2 + 1]: e.tensor_tensor(
                                out=actT[:, jj, hs], in0=silu_t[sk][:], in1=bank[:], op=ALU.mult)),
                                deps=[t_s, mm[1]] + self.act_rd)
                            self.silu_rd[sk] = [t_m]
                            self.pa_rd[half * 2] = [t_s]
                            self.pa_rd[half * 2 + 1] = [t_m]
                            act_ready = t_m
                    self.act_rd = []
                    src_act = actT
                else:
                    act_ready = None
                    src_act = hT
                nfc = pf1 - pf0
                for i in range(KC):
                    k = wring.next()
                    src = w_out[pf0 * P:pf1 * P, i * P:(i + 1) * P].rearrange("(fc p) c -> p fc c", p=P)
                    tl = pg.dma("pool", wring.chan(k),
                                (lambda e, k=k, src=src, n=nfc: e.dma_start(out=wslots[k][:, 0:(1 if DBG_SKIP_W else n), :], in_=src[:, 0:(1 if DBG_SKIP_W else n), :])),
                                deps=[wring.readers[k]])
                    for half in range(NH):
                        hs = slice(half * TT, (half + 1) * TT)
                        xk = half
                        cs = slice(c0 + half * TT, c0 + (half + 1) * TT)
                        if pi_ == 0:
                            srcx = xT[i * P:(i + 1) * P, cs]
                            xdeps = self.xs_rd[xk]
                        else:
                            srcx = rscr[i * P:(i + 1) * P, cs]
                            xdeps = self.xs_rd[xk] + [rs_tok[(i, half)]]
                        t_x = pg.dma("sp", f"xs{xk}" + tag, (lambda e, xk=xk, srcx=srcx: e.dma_start(out=xs[xk][:], in_=srcx)), deps=xdeps)
                        last = None
                        for fc in range(nfc):
                            deps = ([tl, act_ready if F is not None else t_hh[half]] + self.pb_rd[half]) if fc == 0 else []
                            last = pg.op("pe", (lambda e, k=k, fc=fc, half=half, hs=hs, src_act=src_act, nfc=nfc: e.matmul(
                                pb[half][:], lhsT=wslots[k][:, fc, :], rhs=src_act[:, fc, hs], start=(fc == 0), stop=(fc == nfc - 1))),
                                deps=deps, signal=(fc == nfc - 1))
                        wring.readers[k] = last
                        if F is not None:
                            self.act_rd = [last]
                        else:
                            self.hT_rd = [last]
                        if pi_ == 0:
                            t_ax = pg.op("act", (lambda e, xk=xk: e.activation(out=xs[xk][:], in_=xs[xk][:], func=AF.Copy, scale=ALPHA)), deps=[t_x])
                        else:
                            t_ax = t_x
                        t_r = pg.op("dve", (lambda e, xk=xk, half=half: e.scalar_tensor_tensor(
                            out=rb[xk][:], in0=pb[half][:], scalar=float(scale), in1=xs[xk][:], op0=ALU.mult, op1=ALU.add)),
                            deps=[t_ax, last] + self.rb_rd[xk])
                        self.pb_rd[half] = [t_r]
                        self.xs_rd[xk] = [t_r]
                        dstr = rscr[i * P:(i + 1) * P, cs]
                        t_rs = pg.dma("sp", f"rst{xk}" + tag, (lambda e, xk=xk, dstr=dstr: e.dma_start(out=dstr, in_=rb[xk][:])), deps=[t_r])
                        rs_tok[(i, half)] = t_rs
                        self.rb_rd[xk] = [t_rs]
                        if lastp:
                            t_sq = pg.op("act", (lambda e, xk=xk: e.activation(out=sq[xk][:], in_=rb[xk][:], func=AF.Square)),
                                         deps=[t_r] + self.sq_rd[xk])
                            t_st0 = pg.op("pe", (lambda e, xk=xk, i=i, half=half: e.matmul(pst[half][:], lhsT=ones[:], rhs=rb[xk][:], start=(i == 0), stop=(i == KC - 1))),
                                          deps=[t_r, self.t_ones] + (self.pst_rd[half] if i == 0 else []))
                            t_st1 = pg.op("pe", (lambda e, xk=xk, i=i, half=half: e.matmul(pa[half][:], lhsT=ones[:], rhs=sq[xk][:], start=(i == 0), stop=(i == KC - 1))),
                                          deps=[t_sq] + (self.pa_rd[half] if i == 0 else []))
                            if i == 0:
                                self.pst_rd[half] = []
                                self.pa_rd[half] = []
                            self.sq_rd[xk] = [t_st1]
                            self.rb_rd[xk] = [t_sq, t_st0, t_rs]
                            last_stat[half] = t_st1
            invD = 1.0 / D
            for half in range(NH):
                cs = slice(c0 + half * TT, c0 + (half + 1) * TT)
                t_mean = pg.op("act", (lambda e, half=half: e.activation(out=mean[:], in_=pst[half][:], func=AF.Copy, scale=invD)),
                               deps=[last_stat[half]] + self.stat_rd)
                t_msq = pg.op("dve", lambda e: e.tensor_tensor(out=msq[:], in0=mean[:], in1=mean[:], op=ALU.mult),
                              deps=[t_mean] + self.stat_rd)
                t_var = pg.op("dve", (lambda e, half=half: e.scalar_tensor_tensor(out=msq[:], in0=pa[half][:], scalar=invD, in1=msq[:],
                                                                                   op0=ALU.mult, op1=ALU.subtract)), deps=[t_msq, last_stat[half]])
                self.pst_rd[half] = [t_mean]
                self.pa_rd[half] = [t_var]
                t_veps = pg.op("dve", lambda e: e.tensor_scalar(out=msq[:], in0=msq[:], scalar1=float(LN_EPS), scalar2=None, op0=ALU.add),
                               deps=[t_var])
                t_std = pg.op("act", lambda e: e.activation(out=rstd[:], in_=msq[:], func=AF.Sqrt), deps=[t_veps] + self.stat_rd)
                t_rstd = pg.op("dve", lambda e: e.reciprocal(out=rstd[:], in_=rstd[:]), deps=[t_std])
                t_nmr = pg.op("dve", lambda e: e.scalar_tensor_tensor(out=nmr[:], in0=mean[:], scalar=-1.0, in1=rstd[:],
                                                                     op0=ALU.mult, op1=ALU.mult), deps=[t_rstd, t_mean] + self.stat_rd)
                for i in range(KC):
                    xk = i % 2
                    srcr = rscr[i * P:(i + 1) * P, cs]
                    t_l = pg.dma("sp", f"rld{xk}" + tag, (lambda e, xk=xk, srcr=srcr: e.dma_start(out=rb[xk][:], in_=srcr)),
                                 deps=[rs_tok[(i, half)]] + self.rb_rd[xk])
                    t_1 = pg.op("dve", (lambda e, xk=xk: e.tensor_tensor(out=rb[xk][:], in0=rb[xk][:], in1=rstd[:], op=ALU.mult)),
                                deps=[t_l, t_rstd])
                    t_2 = pg.op("dve", (lambda e, xk=xk: e.tensor_tensor(out=rb[xk][:], in0=rb[xk][:], in1=nmr[:], op=ALU.add)),
                                deps=[t_1, t_nmr])
                    t_y = pg.op("act", (lambda e, xk=xk, i=i: e.activation(out=yo[xk][:], in_=rb[xk][:], func=AF.Identity,
                                                                            scale=gsb[:, i:i + 1], bias=bsb[:, i:i + 1])),
                                deps=[t_2, t_g, t_b] + self.yo_rd[xk])
                    dsty = yT[i * P:(i + 1) * P, cs]
                    t_ys = pg.dma("sp", f"yst{xk}" + tag, (lambda e, xk=xk, dsty=dsty: e.dma_start(out=dsty, in_=yo[xk][:])), deps=[t_y])
                    out_tokens.append(t_ys)
                    self.yo_rd[xk] = [t_ys]
                    if ybT is not None:
                        t_yb = pg.op("dve", (lambda e, xk=xk: e.tensor_copy(out=yob[xk][:], in_=yo[xk][:])), deps=[t_y] + self.yob_rd[xk])
                        dstb = ybT[i * P:(i + 1) * P, cs]
                        t_ybs = pg.dma("sp", f"ybst{xk}" + tag, (lambda e, xk=xk, dstb=dstb: e.dma_start(out=dstb, in_=yob[xk][:])), deps=[t_yb])
                        self.yob_rd[xk] = [t_ybs]
                        out_tokens.append(t_ybs)
                        self.yo_rd[xk] = [t_ys, t_yb]
                    if keep_h:
                        t_hw = pg.op("dve", (lambda e, xk=xk, i=i, half=half: e.tensor_copy(out=hT[:, i, half * TT:(half + 1) * TT], in_=yo[xk][:])),
                                     deps=[t_y] + self.hT_rd)
                        self.yo_rd[xk] = self.yo_rd[xk] + [t_hw]
                        self.h_written[half] = t_hw
                    self.rb_rd[xk] = [t_y]
                    self.stat_rd = [t_2]
                    self.gb_rd = [t_y]
            if keep_h:
                self.hT_rd = []
        return out_tokens


NEG = -30000.0


def emit_moba(pg, nc, es, *, D, S, HPC, hb, wqkv, consts, oT, dbg=3):
    KC = D // P
    L = 256
    NBLK = S // L
    NQT = S // P
    NTB = S // TT
    scale = float(P) ** -0.5
    NPM = 5
    SK1, SK2 = 2, 4
    sb = lambda name, shape, dt: es.enter_context(nc.sbuf_tensor("mb_" + name, shape, dt))
    ps = lambda name, shape, dt: es.enter_context(nc.psum_tensor("mb_" + name, shape, dt))
    hTb = [sb(f"hTb{k}", [P, KC, TT], BF16) for k in range(2)]
    wsb = sb("wsb", [P, 3, KC, P], BF16)
    QT = sb("QT", [P, S], BF16)
    KT = sb("KT", [P, S], BF16)
    Vsb = sb("Vsb", [P, S // P, P], BF16)
    kmean_f = sb("kmean_f", [P, NBLK], F32)
    kmean_b = sb("kmean_b", [P, NBLK], BF16)
    identb = sb("identb", [P, P], BF16)
    identf = sb("identf", [P, P], F32)
    triM = sb("triM", [P, P], BF16)
    MB = sb("MB", [P, NBLK, NBLK], F32)
    gate_sb = [sb(f"gate{k}", [P, NBLK], F32) for k in range(2)]
    top8 = [sb(f"top8{k}", [P, 8], F32) for k in range(2)]
    selb = sb("selb", [P, NQT, NBLK], F32)
    Pm = [sb(f"Pm{k}", [P, L], BF16) for k in range(NPM)]
    PTsb = [sb(f"PTsb{k}", [P, 2, P], BF16) for k in range(NPM)]
    NV = NBLK + 8
    rs = [sb(f"rs{k}", [P, NV], F32) for k in range(8)]
    rsum = [sb(f"rsum{k}", [P, 1], F32) for k in range(2)]
    Osb = [sb(f"Osb{k}", [P, P], F32) for k in range(2)]
    oT_sb = sb("oT_sb", [P, S], BF16)
    B = [ps(f"B{k}", [P, TT], F32) for k in range(3)]
    O = [ps(f"O{k}", [P, TT], F32) for k in range(2)]
    PTv = [ps(f"PT{k}", [P, 8, P], BF16) for k in range(2)]

    t_c = pg.dma("sp", "mbc", lambda e: e.dma_start(out=identf[:], in_=consts[:, 0:P]))
    t_c = pg.dma("sp", "mbc", lambda e: e.dma_start(out=MB[:], in_=consts[:, 2 * P:2 * P + NBLK * NBLK].rearrange("p (j n) -> p j n", n=NBLK)))
    t_c2 = pg.dma("pool", "mbc2", lambda e: e.dma_start(out=identb[:], in_=consts[:, 0:P]))
    t_c2 = pg.dma("pool", "mbc2", lambda e: e.dma_start(out=triM[:], in_=consts[:, P:2 * P]))

    B_rd = [[], [], []]
    hT_rd = [[], []]
    w_rd = []
    qkv_rd = []
    O_rd = [[], []]
    PT_rd = [[], []]
    Pm_rd = [[] for _ in range(NPM)]
    PTsb_rd = [[] for _ in range(NPM)]
    rs_rd = [[] for _ in range(8)]
    Osb_rd = [[], []]
    oTsb_rd = []
    out_tokens = []
    hcount = 0
    for h in range(HPC):
        t_w = None
        for which in range(3):
            src = wqkv[which, :, h * P:(h + 1) * P].rearrange("(kc p) c -> p kc c", p=P)
            t_w = pg.dma("pool", "mbw", (lambda e, which=which, src=src: e.dma_start(out=wsb[:, which, :, :], in_=src)), deps=w_rd)
        w_rd = []
        kred = []
        for tb in range(NTB):
            hk = hcount % 2
            hcount += 1
            t_h = None
            nq = 4 if KC >= 4 else 1
            kq = KC // nq
            for q4 in range(nq):
                src = hb[q4 * kq * P:(q4 + 1) * kq * P, tb * TT:(tb + 1) * TT].rearrange("(kc p) t -> p kc t", p=P)
                t_h = pg.dma("sp", f"mbh{hk}", (lambda e, hk=hk, src=src, q4=q4, kq=kq: e.dma_start(out=hTb[hk][:, q4 * kq:(q4 + 1) * kq, :], in_=src)),
                             deps=hT_rd[hk])
            hT_rd[hk] = []
            for which, dst in ((0, QT), (1, KT)):
                if dbg == 11 and which == 1:
                    continue
                bank = B[which]
                last = None
                for kc in range(KC):
                    deps = ([t_w, t_h] + B_rd[which] + qkv_rd) if kc == 0 else []
                    last = pg.op("pe", (lambda e, bank=bank, which=which, kc=kc, hk=hk: e.matmul(
                        bank[:], lhsT=wsb[:, which, kc, :], rhs=hTb[hk][:, kc, :], start=(kc == 0), stop=(kc == KC - 1))),
                        deps=deps, signal=(kc == KC - 1))
                if which == 0:
                    t_cp = pg.op("act", (lambda e, bank=bank, dst=dst, tb=tb: e.activation(out=dst[:, tb * TT:(tb + 1) * TT], in_=bank[:], func=AF.Copy)),
                                 deps=[last] + qkv_rd)
                else:
                    for bl in range(TT // L):
                        t_cp = pg.op("act", (lambda e, bank=bank, dst=dst, tb=tb, bl=bl: e.activation(
                            out=dst[:, tb * TT + bl * L:tb * TT + (bl + 1) * L], in_=bank[:, bl * L:(bl + 1) * L], func=AF.Copy,
                            accum_out=kmean_f[:, tb * 2 + bl:tb * 2 + bl + 1])), deps=[last] + qkv_rd)
                    kred.append(t_cp)
                rd = [t_cp]
                B_rd[which] = rd
            last = None
            if dbg in (11, 12, 13):
                hT_rd[hk] = [t_cp]
                w_rd = [t_cp]
                proj_done = [t_cp]
                continue
            for c in range(TT // P):
                for kc in range(KC):
                    deps = (B_rd[2] + qkv_rd) if (kc == 0 and c == 0) else []
                    last = pg.op("pe", (lambda e, c=c, kc=kc, hk=hk: e.matmul(
                        B[2][:, c * P:(c + 1) * P], lhsT=hTb[hk][:, kc, c * P:(c + 1) * P], rhs=wsb[:, 2, kc, :],
                        start=(kc == 0), stop=(kc == KC - 1))),
                        deps=deps, signal=(kc == KC - 1 and c == TT // P - 1))
            t_v = pg.op("dve", (lambda e, tb=tb: e.tensor_copy(out=Vsb[:, tb * 4:(tb + 1) * 4, :], in_=B[2][:].rearrange("p (c e) -> p c e", e=P))),
                        deps=[last] + qkv_rd)
            B_rd[2] = [t_v]
            hT_rd[hk] = [last]
            w_rd = [last]
            proj_done = [t_cp, t_v]
        qkv_rd = []
        if dbg in (1, 11, 12, 13):
            t_o = pg.dma("sp", "mbo", (lambda e, h=h: e.dma_start(out=oT[h * P:(h + 1) * P, :], in_=QT[:, :])), deps=proj_done)
            out_tokens.append(t_o)
            qkv_rd = [t_o]
            continue
        t_kb = pg.op("dve", lambda e: e.tensor_scalar(out=kmean_b[:], in0=kmean_f[:], scalar1=1.0 / L, scalar2=None, op0=ALU.mult),
                     deps=kred)
        sel_done = None
        gbanks = [B[0], B[1], O[0], O[1]]
        gnames = [(B_rd, 0), (B_rd, 1), (O_rd, 0), (O_rd, 1)]
        QPB = TT // NBLK
        assert NQT <= 4 * QPB
        t_gm = [None] * NQT
        for qt in range(NQT):
            gb, gi = qt // QPB, qt % QPB
            lst, li = gnames[gb]
            t_gm[qt] = pg.op("pe", (lambda e, qt=qt, gb=gb, gi=gi: e.matmul(gbanks[gb][:, gi * NBLK:(gi + 1) * NBLK], lhsT=QT[:, qt * P:(qt + 1) * P],
                                                                            rhs=kmean_b[:, :], start=True, stop=True)),
                               deps=([t_kb] + proj_done + lst[li]) if gi == 0 else [])
        for qt in range(NQT):
            j = qt // 2
            gk = qt % 2
            gb, gi = qt // QPB, qt % QPB
            t_gs = pg.op("dve", (lambda e, gk=gk, j=j, gb=gb, gi=gi: e.tensor_tensor(out=gate_sb[gk][:], in0=gbanks[gb][:, gi * NBLK:(gi + 1) * NBLK],
                                                                                      in1=MB[:, j, :], op=ALU.add)),
                         deps=[t_gm[min(NQT - 1, (gb + 1) * QPB - 1)], t_c])
            lst, li = gnames[gb]
            lst[li] = [t_gs]
            t_t8 = pg.op("dve", (lambda e, gk=gk: e.max(out=top8[gk][:], in_=gate_sb[gk][:])), deps=[t_gs])
            t_s1 = pg.op("dve", (lambda e, gk=gk, qt=qt: e.tensor_scalar(out=selb[:, qt, :], in0=gate_sb[gk][:], scalar1=top8[gk][:, 2:3], scalar2=1.0,
                                                                          op0=ALU.is_ge, op1=ALU.subtract)), deps=[t_t8])
            sel_done = pg.op("dve", (lambda e, qt=qt: e.tensor_scalar(out=selb[:, qt, :], in0=selb[:, qt, :], scalar1=-NEG, scalar2=None, op0=ALU.mult)),
                             deps=[t_s1])
        if dbg == 2:
            t_o = pg.dma("sp", "mbo", (lambda e, h=h: e.dma_start(out=oT[h * P:(h + 1) * P, :], in_=QT[:, :])), deps=proj_done + [sel_done])
            out_tokens.append(t_o)
            qkv_rd = [t_o]
            continue
        visits = []
        for qt in range(NQT):
            j, half = qt // 2, qt % 2
            vl = []
            for n in range(j):
                vl.append(("past", n, 2 * n, 2))
            if half == 0:
                vl.append(("diag", j, 2 * j, 1))
            else:
                vl.append(("full", j, 2 * j, 1))
                vl.append(("diag", j, 2 * j + 1, 1))
            for idx, v in enumerate(vl):
                visits.append((qt, idx == 0, idx == len(vl) - 1, v[0], v[1], v[2], v[3], idx))
        nv = len(visits)
        st = [dict() for _ in range(nv)]

        def do_qk(s):
            qt, first, lastv, kind, n, ch0, nch, idx = visits[s]
            bk = s % 2
            W = nch * P
            t = pg.op("pe", (lambda e: e.matmul(B[bk][:, 0:W], lhsT=QT[:, qt * P:(qt + 1) * P], rhs=KT[:, ch0 * P:ch0 * P + W],
                                                start=True, stop=(kind != "diag"))),
                      deps=B_rd[bk] + [sel_done] + proj_done, signal=(kind != "diag"))
            if kind == "diag":
                t = pg.op("pe", (lambda e: e.matmul(B[bk][:, 0:W], lhsT=identb[:], rhs=triM[:], start=False, stop=True)), deps=[t_c2])
            st[s]["qk"] = t

        def do_exp(s):
            qt, first, lastv, kind, n, ch0, nch, idx = visits[s]
            bk = s % 2
            pk = s % NPM
            rk = qt % 8
            W = nch * P
            if first:
                st[s]["rsz"] = pg.op("dve", (lambda e: e.memset(rs[rk][:], 0.0)), deps=rs_rd[rk])
                rs_rd[rk] = []
            deps = [st[s]["qk"]] + Pm_rd[pk] + ([st[s]["rsz"]] if first else [])
            if kind == "past":
                t = pg.op("act", (lambda e: e.activation(out=Pm[pk][:, 0:W], in_=B[bk][:, 0:W], func=AF.Exp, scale=scale,
                                                         bias=selb[:, qt, n:n + 1], accum_out=rs[rk][:, idx:idx + 1])), deps=deps)
            else:
                t = pg.op("act", (lambda e: e.activation(out=Pm[pk][:, 0:W], in_=B[bk][:, 0:W], func=AF.Exp, scale=scale,
                                                         accum_out=rs[rk][:, idx:idx + 1])), deps=deps)
            B_rd[bk] = [t]
            st[s]["exp"] = t

        def do_tr(s):
            qt, first, lastv, kind, n, ch0, nch, idx = visits[s]
            pk = s % NPM
            tk = s % 2
            t = None
            for c in range(nch):
                t = pg.op("pe", (lambda e, c=c: e.transpose(PTv[tk][:, c, :], Pm[pk][:, c * P:(c + 1) * P], identb[:])),
                          deps=([st[s]["exp"], t_c2] + PT_rd[tk]) if c == 0 else [], signal=(c == nch - 1))
            Pm_rd[pk] = [t]
            t2 = pg.op("dve", (lambda e: e.tensor_copy(out=PTsb[pk][:, 0:nch, :], in_=PTv[tk][:, 0:nch, :])), deps=[t] + PTsb_rd[pk])
            PT_rd[tk] = [t2]
            st[s]["cp"] = t2

        def do_pv(s):
            qt, first, lastv, kind, n, ch0, nch, idx = visits[s]
            pk = s % NPM
            ok = qt % 2
            t = None
            for c in range(nch):
                fst = first and c == 0
                lst = lastv and c == nch - 1
                t = pg.op("pe", (lambda e, c=c, fst=fst, lst=lst: e.matmul(O[ok][:, 0:P], lhsT=PTsb[pk][:, c, :], rhs=Vsb[:, ch0 + c, :],
                                                                            start=fst, stop=lst)),
                          deps=([st[s]["cp"]] + (O_rd[ok] if fst else [])) if c == 0 else [], signal=(c == nch - 1))
            PTsb_rd[pk] = [t]
            if lastv:
                rk = qt % 2
                rq = qt % 8
                t_rs = pg.op("dve", (lambda e: e.tensor_scalar(out=rs[rq][:], in0=rs[rq][:], scalar1=1.0, scalar2=0.0, op0=ALU.mult, op1=ALU.add,
                                                               accum_out=rsum[rk][:])), deps=[st[s]["exp"]])
                t_ri = pg.op("dve", (lambda e: e.reciprocal(out=rsum[rk][:], in_=rsum[rk][:])), deps=[t_rs])
                rs_rd[rq] = [t_rs]
                t_on = pg.op("dve", (lambda e: e.tensor_scalar(out=Osb[ok][:], in0=O[ok][:, 0:P], scalar1=rsum[rk][:, 0:1], scalar2=None, op0=ALU.mult)),
                             deps=[t, t_ri] + Osb_rd[ok])
                O_rd[ok] = [t_on]
                t_tr = pg.op("pe", (lambda e: e.transpose(B[2][:, 0:P], Osb[ok][:], identf[:])), deps=[t_on, t_c] + B_rd[2])
                Osb_rd[ok] = [t_tr]
                t_oc = pg.op("act", (lambda e: e.activation(out=oT_sb[:, qt * P:(qt + 1) * P], in_=B[2][:, 0:P], func=AF.Copy)),
                             deps=[t_tr] + (oTsb_rd if qt == 0 else []))
                B_rd[2] = [t_oc]
                st[s]["fin"] = t_oc

        for s in range(nv + SK2):
            if s < nv:
                do_qk(s)
                do_exp(s)
            if 0 <= s - SK1 < nv:
                do_tr(s - SK1)
            if 0 <= s - SK2 < nv:
                do_pv(s - SK2)
        fin = st[nv - 1]["fin"]
        qkv_rd = [fin, st[nv - 1]["qk"]]
        t_o = pg.dma("sp", "mbo", (lambda e, h=h: e.dma_start(out=oT[h * P:(h + 1) * P, :], in_=oT_sb[:, :])), deps=[fin])
        oTsb_rd = [t_o]
        out_tokens.append(t_o)
    return out_tokens


def prog_check(pg):
    done = {}
    pos = {e: 0 for e in pg.ENGS}
    progress = True
    while progress:
        progress = False
        for e in pg.ENGS:
            ops = pg.ops[e]
            while pos[e] < len(ops):
                fn, deps, tok = ops[pos[e]]
                if all(done.get(src, 0) >= val for (src, val) in deps):
                    if tok is not None:
                        done[tok[0]] = max(done.get(tok[0], 0), tok[1])
                    pos[e] += 1
                    progress = True
                else:
                    break
    stuck = {e: (pos[e], len(pg.ops[e])) for e in pg.ENGS if pos[e] < len(pg.ops[e])}
    for e, (p_, n) in stuck.items():
        fn, deps, tok = pg.ops[e][p_]
        print("STUCK", e, p_, n, "deps", deps, "have", {s: done.get(s, 0) for s, _ in deps})
    return not stuck


def emit_hgrn(pg, nc, es, *, D, S, HPC, hb, w4, hgp, consts, oT):
    KC = D // P
    C = 64
    NCH = TT // C
    NTB = S // TT
    sb = lambda name, shape, dt: es.enter_context(nc.sbuf_tensor("hg_" + name, shape, dt))
    ps = lambda name, shape, dt: es.enter_context(nc.psum_tensor("hg_" + name, shape, dt))
    hTb = [sb(f"hTb{k}", [P, KC, TT], BF16) for k in range(2)]
    wsb = sb("wsb", [P, 4, KC, P], BF16)
    hgp_sb = sb("hgp", [P, HPC, 4], F32)
    lbx = sb("lbx", [P, HPC, 3], F32)
    lbm = sb("lbm", [P, HPC], F32)
    lbs = sb("lbs", [P, HPC], F32)
    lb = sb("lb", [P, HPC], F32)
    oml = sb("oml", [P, HPC], F32)
    identb = sb("identb", [P, P], BF16)
    onesf = sb("onesf", [P, P], F32)
    ones64 = sb("ones64", [P, C], F32)
    maskT = sb("maskT", [P, C], F32)
    f32t = lambda n: sb(n, [P, TT], F32)
    qs, sg, fg, lf, kk, sgg, cum, e1, e2, e3, e4, osq, ms, t1 = [f32t(n) for n in
        ("qs", "sg", "fg", "lf", "kk", "sgg", "cum", "e1", "e2", "e3", "e4", "osq", "ms", "t1")]
    negmid = sb("negmid", [P, NCH], F32)
    el = sb("el", [P, NCH], F32)
    qt_bf, kt_bf, qh_bf, kh_bf, vT_bf = [sb(n, [P, TT], BF16) for n in ("qt_bf", "kt_bf", "qh_bf", "kh_bf", "vT_bf")]
    khT_sb = sb("khT_sb", [P, NCH, P], BF16)
    vtm_sb = sb("vtm_sb", [P, NCH, P], BF16)
    scm = [sb(f"scm{k}", [P, C], BF16) for k in range(2)]
    Sst = [sb(f"S{k}", [P, P], F32) for k in range(2)]
    S_bf = [sb(f"Sbf{k}", [P, P], BF16) for k in range(2)]
    out_sb = sb("out_sb", [P, S], BF16)
    PA = ps("PA", [P, TT], F32)
    PB = ps("PB", [P, TT], F32)
    TR = [ps(f"TR{k}", [P, NCH, P], BF16) for k in range(2)]
    Ob = ps("Ob", [P, TT], F32)
    SC = ps("SC", [P, TT], F32)
    KV = [ps(f"KV{k}", [P, TT], F32) for k in range(2)]
    cum3 = cum[:].rearrange("p (c t) -> p c t", t=C)

    t_c1 = pg.dma("pool", "hgc", lambda e: e.dma_start(out=identb[:], in_=consts[:, 0:P]))
    t_c2 = pg.dma("sp", "hgc2", lambda e: e.dma_start(out=maskT[:], in_=consts[:, P:P + C]))
    t_c3 = pg.dma("sp", "hgc3", lambda e: e.dma_start(out=hgp_sb[:], in_=hgp[:, :].rearrange("p (h f) -> p h f", f=4)))
    t_o1 = pg.op("dve", lambda e: e.memset(onesf[:], 1.0))
    t_o2 = pg.op("dve", lambda e: e.memset(ones64[:], 1.0))
    t_z = pg.op("dve", lambda e: e.memset(khT_sb[:], 0.0))
    t_z = pg.op("dve", lambda e: e.memset(vtm_sb[:], 0.0))
    t_z = pg.op("dve", lambda e: e.memset(scm[0][:], 0.0))
    t_z = pg.op("dve", lambda e: e.memset(scm[1][:], 0.0))
    t = pg.op("dve", lambda e: e.tensor_tensor(out=lbm[:], in0=hgp_sb[:, :, 0], in1=hgp_sb[:, :, 1], op=ALU.max), deps=[t_c3])
    t = pg.op("dve", lambda e: e.tensor_tensor(out=lbm[:], in0=lbm[:], in1=hgp_sb[:, :, 2], op=ALU.max), deps=[t])
    for r in range(3):
        t = pg.op("dve", (lambda e, r=r: e.tensor_tensor(out=lbx[:, :, r], in0=hgp_sb[:, :, r], in1=lbm[:], op=ALU.subtract)), deps=[t])
    t = pg.op("act", lambda e: e.activation(out=lbx[:], in_=lbx[:], func=AF.Exp), deps=[t])
    t = pg.op("dve", lambda e: e.tensor_tensor(out=lbs[:], in0=lbx[:, :, 0], in1=lbx[:, :, 1], op=ALU.add), deps=[t])
    t = pg.op("dve", lambda e: e.tensor_tensor(out=lbs[:], in0=lbs[:], in1=lbx[:, :, 2], op=ALU.add), deps=[t])
    t = pg.op("dve", lambda e: e.reciprocal(out=lbs[:], in_=lbs[:]), deps=[t])
    t = pg.op("dve", lambda e: e.tensor_tensor(out=lb[:], in0=lbx[:, :, 0], in1=lbs[:], op=ALU.mult), deps=[t])
    t_lb = pg.op("dve", lambda e: e.tensor_scalar(out=oml[:], in0=lb[:], scalar1=-1.0, scalar2=1.0, op0=ALU.mult, op1=ALU.add), deps=[t])

    rd = {k: [] for k in ("PA", "PB", "TR0", "TR1", "Ob", "SC", "KV0", "KV1", "hT0", "hT1", "w", "qs", "sg", "fg", "lf", "kk",
                          "sgg", "cum", "e1", "e2", "e3", "e4", "osq", "ms", "t1", "negmid", "el", "qt", "kt", "qh", "kh", "vT",
                          "khT", "vtm", "scm0", "scm1", "S0", "S1", "Sbf0", "Sbf1", "out")}
    out_tokens = []
    hcount = 0
    nstate = 0
    for h in range(HPC):
        t_w = None
        for which in range(4):
            src = w4[which, :, h * P:(h + 1) * P].rearrange("(kc p) c -> p kc c", p=P)
            t_w = pg.dma("pool", "hgw", (lambda e, which=which, src=src: e.dma_start(out=wsb[:, which, :, :], in_=src)), deps=rd["w"])
        rd["w"] = []
        s0 = nstate % 2
        t_s0 = pg.op("dve", (lambda e, s0=s0: e.memset(Sst[s0][:], 0.0)), deps=rd[f"S{s0}"])
        t_sb0 = pg.op("dve", (lambda e, s0=s0: e.memset(S_bf[s0][:], 0.0)), deps=rd[f"Sbf{s0}"])
        S_ready = t_s0
        Sbf_ready = t_sb0
        rd[f"S{s0}"] = []
        rd[f"Sbf{s0}"] = []
        def load_h(tb_):
            hk_ = (hbase + tb_) % 2
            t_ = None
            nq = 4 if KC >= 4 else 1
            kq = KC // nq
            for q4 in range(nq):
                src = hb[q4 * kq * P:(q4 + 1) * kq * P, tb_ * TT:(tb_ + 1) * TT].rearrange("(kc p) t -> p kc t", p=P)
                t_ = pg.dma("sp", f"hgh{hk_}", (lambda e, hk_=hk_, src=src, q4=q4, kq=kq: e.dma_start(out=hTb[hk_][:, q4 * kq:(q4 + 1) * kq, :], in_=src)),
                            deps=rd[f"hT{hk_}"])
            rd[f"hT{hk_}"] = []
            return hk_, t_

        def proj_thunks(which, bank, bname, hk_, t_h_):
            res = []
            for kc in range(KC):
                def th(kc=kc):
                    deps = ([t_w, t_h_] + rd[bname]) if kc == 0 else []
                    tk = pg.op("pe", (lambda e, kc=kc: e.matmul(bank[:], lhsT=wsb[:, which, kc, :], rhs=hTb[hk_][:, kc, :],
                                                               start=(kc == 0), stop=(kc == KC - 1))),
                               deps=deps, signal=(kc == KC - 1))
                    if kc == KC - 1:
                        rd[bname] = []
                    return tk
                res.append(th)
            return res

        hbase = hcount
        hcount += NTB
        cur_h = load_h(0)
        pend = proj_thunks(0, PA, "PA", cur_h[0], cur_h[1]) + proj_thunks(1, PB, "PB", cur_h[0], cur_h[1])
        r1 = [th() for th in pend]
        m_q, m_f = r1[KC - 1], r1[2 * KC - 1]
        for tb in range(NTB):
            hk, t_h = cur_h
            nxt_h = load_h(tb + 1) if tb + 1 < NTB else None

            def proj(which, bank, bname):
                last = None
                for th in proj_thunks(which, bank, bname, hk, t_h):
                    last = th()
                return last
            a_q = pg.op("act", lambda e: e.activation(out=qs[:], in_=PA[:], func=AF.Silu), deps=[m_q] + rd["qs"])
            rd["qs"] = []
            rd["PA"] = [a_q]
            a_sg = pg.op("act", lambda e: e.activation(out=sg[:], in_=PB[:], func=AF.Sigmoid), deps=[m_f] + rd["sg"])
            rd["sg"] = []
            rd["PB"] = [a_sg]
            d_fg = pg.op("dve", (lambda e, h=h: e.tensor_scalar(out=fg[:], in0=sg[:], scalar1=oml[:, h:h + 1], scalar2=lb[:, h:h + 1],
                                                                op0=ALU.mult, op1=ALU.add)), deps=[a_sg, t_lb] + rd["fg"])
            rd["fg"] = []
            rd["sg"] = [d_fg]
            a_lf = pg.op("act", lambda e: e.activation(out=lf[:], in_=fg[:], func=AF.Ln), deps=[d_fg] + rd["lf"])
            rd["lf"] = []
            d_kk = pg.op("dve", lambda e: e.tensor_scalar(out=kk[:], in0=fg[:], scalar1=-1.0, scalar2=1.0, op0=ALU.mult, op1=ALU.add),
                         deps=[d_fg] + rd["kk"])
            rd["kk"] = []
            rd["fg"] = [a_lf, d_kk]
            m_i = proj(2, PA, "PA")
            m_g = proj(3, PB, "PB")
            rd[f"hT{hk}"] = [m_g]
            rd["w"] = [m_g]
            a_v = pg.op("act", lambda e: e.activation(out=vT_bf[:], in_=PA[:], func=AF.Copy), deps=[m_i] + rd["vT"])
            rd["vT"] = []
            rd["PA"] = [a_v]
            a_g = pg.op("act", lambda e: e.activation(out=sgg[:], in_=PB[:], func=AF.Silu), deps=[m_g] + rd["sgg"])
            rd["sgg"] = []
            rd["PB"] = [a_g]
            d_cum = None
            for c in range(NCH):
                d_cum = pg.op("dve", (lambda e, c=c: e.tensor_tensor_scan(out=cum[:, c * C:(c + 1) * C], data0=ones64[:, :], data1=lf[:, c * C:(c + 1) * C],
                                                                         initial=0.0, op0=ALU.mult, op1=ALU.add)),
                              deps=([a_lf, t_o2] + rd["cum"]) if c == 0 else [])
            rd["cum"] = []
            rd["lf"] = [d_cum]
            d_nm = pg.op("dve", lambda e: e.tensor_scalar(out=negmid[:], in0=cum3[:, :, C // 2 - 1], scalar1=-1.0, scalar2=None, op0=ALU.mult),
                         deps=[d_cum] + rd["negmid"])
            rd["negmid"] = []
            a_el = pg.op("act", lambda e: e.activation(out=el[:], in_=cum3[:, :, C - 1], func=AF.Exp), deps=[d_cum] + rd["el"])
            rd["el"] = []
            a_e3 = pg.op("act", lambda e: e.activation(out=e3[:], in_=cum[:], func=AF.Exp), deps=[d_cum] + rd["e3"])
            rd["e3"] = []
            a_e1 = a_e2 = a_e4 = None
            for c in range(NCH):
                sl = slice(c * C, (c + 1) * C)
                a_e1 = pg.op("act", (lambda e, c=c, sl=sl: e.activation(out=e1[:, sl], in_=cum[:, sl], func=AF.Exp, bias=negmid[:, c:c + 1], scale=1.0)),
                             deps=([d_nm] + rd["e1"]) if c == 0 else [])
            for c in range(NCH):
                sl = slice(c * C, (c + 1) * C)
                a_e2 = pg.op("act", (lambda e, c=c, sl=sl: e.activation(out=e2[:, sl], in_=cum[:, sl], func=AF.Exp, bias=cum3[:, c, C // 2 - 1:C // 2], scale=-1.0)),
                             deps=([d_cum] + rd["e2"]) if c == 0 else [])
            for c in range(NCH):
                sl = slice(c * C, (c + 1) * C)
                a_e4 = pg.op("act", (lambda e, c=c, sl=sl: e.activation(out=e4[:, sl], in_=cum[:, sl], func=AF.Exp, bias=cum3[:, c, C - 1:C], scale=-1.0)),
                             deps=([d_cum] + rd["e4"]) if c == 0 else [])
            rd["e1"] = []; rd["e2"] = []; rd["e4"] = []
            rd["negmid"] = [a_e1]
            rd["cum"] = [a_e4, a_e3, a_el, d_nm]
            d_qt = pg.op("dve", lambda e: e.tensor_tensor(out=qt_bf[:], in0=qs[:], in1=e1[:], op=ALU.mult), deps=[a_q, a_e1] + rd["qt"])
            d_kt = pg.op("dve", lambda e: e.tensor_tensor(out=kt_bf[:], in0=kk[:], in1=e2[:], op=ALU.mult), deps=[d_kk, a_e2] + rd["kt"])
            d_qh = pg.op("dve", lambda e: e.tensor_tensor(out=qh_bf[:], in0=qs[:], in1=e3[:], op=ALU.mult), deps=[a_q, a_e3] + rd["qh"])
            d_kh = pg.op("dve", lambda e: e.tensor_tensor(out=kh_bf[:], in0=kk[:], in1=e4[:], op=ALU.mult), deps=[d_kk, a_e4] + rd["kh"])
            rd["qt"] = []; rd["kt"] = []; rd["qh"] = []; rd["kh"] = []
            rd["qs"] = [d_qt, d_qh]
            rd["kk"] = [d_kt, d_kh]
            rd["e1"] = [d_qt]; rd["e2"] = [d_kt]; rd["e3"] = [d_qh]; rd["e4"] = [d_kh]
            p_t0 = p_t1 = None
            for c in range(NCH):
                p_t0 = pg.op("pe", (lambda e, c=c: e.transpose(TR[0][0:C, c, :], kh_bf[:, c * C:(c + 1) * C], identb[:])),
                             deps=([d_kh, t_c1] + rd["TR0"]) if c == 0 else [], signal=(c == NCH - 1))
            for c in range(NCH):
                p_t1 = pg.op("pe", (lambda e, c=c: e.transpose(TR[1][0:C, c, :], vT_bf[:, c * C:(c + 1) * C], identb[:])),
                             deps=([a_v, t_c1] + rd["TR1"]) if c == 0 else [], signal=(c == NCH - 1))
            rd["kh"] = [p_t0]
            rd["vT"] = [p_t1]
            d_c0 = pg.op("dve", lambda e: e.tensor_copy(out=khT_sb[0:C, :, :], in_=TR[0][0:C, :, :]), deps=[p_t0, t_z] + rd["khT"])
            a_c1 = pg.op("act", lambda e: e.activation(out=vtm_sb[0:C, :, :], in_=TR[1][0:C, :, :], func=AF.Copy), deps=[p_t1, t_z] + rd["vtm"])
            rd["TR0"] = [d_c0]
            rd["TR1"] = [a_c1]
            rd["khT"] = []; rd["vtm"] = []
            p_o = None
            if nxt_h is not None:
                pend = proj_thunks(0, PA, "PA", nxt_h[0], nxt_h[1]) + proj_thunks(1, PB, "PB", nxt_h[0], nxt_h[1])
            else:
                pend = []
            pend_tok = []
            per = -(-len(pend) // NCH) if pend else 0
            for c in range(NCH):
                sl = slice(c * C, (c + 1) * C)
                ck = c % 2
                p_sc = pg.op("pe", (lambda e, sl=sl: e.matmul(SC[0:C, 0:C], lhsT=kt_bf[:, sl], rhs=qt_bf[:, sl], start=True, stop=True)),
                             deps=[d_kt, d_qt] + rd["SC"])
                d_sc = pg.op("dve", (lambda e, ck=ck: e.tensor_tensor(out=scm[ck][0:C, :], in0=SC[0:C, 0:C], in1=maskT[0:C, :], op=ALU.mult)),
                             deps=[p_sc, t_c2, t_z] + rd[f"scm{ck}"])
                rd["SC"] = [d_sc]
                sk = nstate % 2
                p_o1 = pg.op("pe", (lambda e, c=c, sl=sl, ck=ck: e.matmul(Ob[:, sl], lhsT=vtm_sb[:, c, :], rhs=scm[ck][:, :], start=True, stop=False)),
                             deps=[d_sc, a_c1] + (rd["Ob"] if c == 0 else []), signal=False)
                p_o = pg.op("pe", (lambda e, sl=sl, sk=sk: e.matmul(Ob[:, sl], lhsT=S_bf[sk][:, :], rhs=qh_bf[:, sl], start=False, stop=True)),
                            deps=[Sbf_ready, d_qh])
                rd[f"scm{ck}"] = [p_o]
                rd[f"Sbf{sk}"] = [p_o]
                kvk = c % 2
                p_kv = pg.op("pe", (lambda e, c=c, kvk=kvk: e.matmul(KV[kvk][:, 0:P], lhsT=khT_sb[:, c, :], rhs=vtm_sb[:, c, :], start=True, stop=True)),
                             deps=[d_c0, a_c1] + rd[f"KV{kvk}"])
                sn = (nstate + 1) % 2
                d_S = pg.op("dve", (lambda e, c=c, sk=sk, sn=sn, kvk=kvk: e.scalar_tensor_tensor(
                    out=Sst[sn][:], in0=Sst[sk][:], scalar=el[:, c:c + 1], in1=KV[kvk][:, 0:P], op0=ALU.mult, op1=ALU.add)),
                    deps=[p_kv, a_el, S_ready] + rd[f"S{sn}"])
                rd[f"KV{kvk}"] = [d_S]
                rd[f"S{sn}"] = []
                a_Sb = pg.op("act", (lambda e, sn=sn: e.activation(out=S_bf[sn][:], in_=Sst[sn][:], func=AF.Copy)), deps=[d_S] + rd[f"Sbf{sn}"])
                rd[f"Sbf{sn}"] = []
                rd[f"S{sk}"] = [d_S]
                rd[f"S{sn}"] = [a_Sb]
                S_ready = d_S
                Sbf_ready = a_Sb
                nstate += 1
                for th in pend[c * per:(c + 1) * per]:
                    pend_tok.append(th())
            if pend:
                m_q, m_f = pend_tok[KC - 1], pend_tok[2 * KC - 1]
                cur_h = nxt_h
            rd["khT"] = [p_kv]
            rd["vtm"] = [p_kv, p_o]
            rd["kt"] = [p_sc]; rd["qt"] = [p_sc]; rd["qh"] = [p_o]
            rd["el"] = [d_S]
            a_sq = pg.op("act", lambda e: e.activation(out=osq[:], in_=Ob[:], func=AF.Square), deps=[p_o] + rd["osq"])
            p_ss = pg.op("pe", lambda e: e.matmul(SC[:], lhsT=onesf[:], rhs=osq[:], start=True, stop=True), deps=[a_sq, t_o1] + rd["SC"])
            rd["osq"] = [p_ss]
            d_ms = pg.op("dve", lambda e: e.tensor_scalar(out=ms[:], in0=SC[:], scalar1=1.0 / P, scalar2=float(RMS_EPS), op0=ALU.mult, op1=ALU.add),
                         deps=[p_ss] + rd["ms"])
            rd["SC"] = [d_ms]
            a_sd = pg.op("act", lambda e: e.activation(out=ms[:], in_=ms[:], func=AF.Sqrt), deps=[d_ms])
            d_ri = pg.op("dve", lambda e: e.reciprocal(out=ms[:], in_=ms[:]), deps=[a_sd])
            d_t1 = pg.op("dve", lambda e: e.tensor_tensor(out=t1[:], in0=Ob[:], in1=ms[:], op=ALU.mult), deps=[d_ri, p_o] + rd["t1"])
            rd["Ob"] = [d_t1, a_sq]
            d_ob = pg.op("dve", (lambda e, h=h, tb=tb: e.scalar_tensor_tensor(out=out_sb[:, tb * TT:(tb + 1) * TT], in0=t1[:], scalar=hgp_sb[:, h, 3:4], in1=sgg[:],
                                                                              op0=ALU.mult, op1=ALU.mult)), deps=[d_t1, a_g] + (rd["out"] if tb == 0 else []))
            rd["ms"] = [d_t1]
            rd["t1"] = [d_ob]
            rd["sgg"] = [d_ob]
        t_o = pg.dma("sp", "hgo", (lambda e, h=h: e.dma_start(out=oT[h * P:(h + 1) * P, :], in_=out_sb[:, :])), deps=[d_ob])
        rd["out"] = [t_o]
        out_tokens.append(t_o)
    return out_tokens


import contextlib

NCORES = 8
TPC = SEQ // NCORES
HPC = 4
KC_ = D_MODEL // P


def _ln_lay(v):
    return np.ascontiguousarray(np.asarray(v, np.float32).reshape(-1, P).T)


def build_token_launch(stages):
    nc = bass.Bass("TRN2", target_bir_lowering=False)
    D, T, F = D_MODEL, TPC, D_FF
    dt = lambda name, shape, dty, kind: nc.dram_tensor(name, shape, dty, kind=kind).ap()
    xT = dt("xT", [D, T], F32, "ExternalInput")
    rscr = dt("rscr", [D, T], F32, "Internal")
    pg = Prog(nc)
    with contextlib.ExitStack() as es:
        st = ProjLN3(pg, nc, es, D)
        cur = xT
        prev_keep = False
        toks = ()
        for si, sg in enumerate(stages):
            lastst = si == len(stages) - 1
            nm = sg["name"]
            gT = dt(nm + "_g", [P, D // P], F32, "ExternalInput")
            bT = dt(nm + "_b", [P, D // P], F32, "ExternalInput")
            if lastst:
                yT = dt("yT", [D, T], F32, "ExternalOutput")
                ybT = dt("ybT", [D, T], BF16, "ExternalOutput") if sg.get("want_b") else None
            else:
                yT = dt(f"y_int{si}", [D, T], F32, "Internal")
                ybT = None
            keep = (not lastst) and stages[si + 1]["kind"] == "ffn"
            if sg["kind"] == "ffn":
                w_in = dt(nm + "_win", [D, 2 * F], F32, "ExternalInput")
                w_out = dt(nm + "_wout", [F, D], F32, "ExternalInput")
                toks = st.run(T=T, K2=F, xT=cur, w_out=w_out, gT=gT, bT=bT, yT=yT, ybT=ybT, rscr=rscr, scale=0.5,
                              F=F, w_in=w_in, in_deps=toks, h_resident=prev_keep, keep_h=keep)
            else:
                oin = dt("oin", [D, T], BF16, "ExternalInput")
                w_out = dt(nm + "_wout", [D, D], F32, "ExternalInput")
                toks = st.run(T=T, K2=D, xT=cur, w_out=w_out, gT=gT, bT=bT, yT=yT, ybT=ybT, rscr=rscr, scale=1.0,
                              inT=oin, in_deps=toks, keep_h=keep)
            prev_keep = keep
            cur = yT
        pg.emit(toks)
    return nc


def build_hgrn_launch():
    nc = bass.Bass("TRN2", target_bir_lowering=False)
    D, S = D_MODEL, SEQ
    hb = nc.dram_tensor("hb", [D, S], BF16, kind="ExternalInput").ap()
    w4 = nc.dram_tensor("w4", [4, D, HPC * P], F32, kind="ExternalInput").ap()
    hgp = nc.dram_tensor("hgp", [P, HPC * 4], F32, kind="ExternalInput").ap()
    consts = nc.dram_tensor("consts", [P, P + 64], F32, kind="ExternalInput").ap()
    oT = nc.dram_tensor("oT", [HPC * P, S], BF16, kind="ExternalOutput").ap()
    pg = Prog(nc)
    with contextlib.ExitStack() as es:
        toks = emit_hgrn(pg, nc, es, D=D, S=S, HPC=HPC, hb=hb, w4=w4, hgp=hgp, consts=consts, oT=oT)
        pg.emit(toks)
    return nc


def build_moba_launch():
    nc = bass.Bass("TRN2", target_bir_lowering=False)
    D, S = D_MODEL, SEQ
    NBLK = S // 256
    hb = nc.dram_tensor("hb", [D, S], BF16, kind="ExternalInput").ap()
    wqkv = nc.dram_tensor("wqkv", [3, D, HPC * P], F32, kind="ExternalInput").ap()
    consts = nc.dram_tensor("consts", [P, 2 * P + NBLK * NBLK], F32, kind="ExternalInput").ap()
    oT = nc.dram_tensor("oT", [HPC * P, S], BF16, kind="ExternalOutput").ap()
    pg = Prog(nc)
    with contextlib.ExitStack() as es:
        toks = emit_moba(pg, nc, es, D=D, S=S, HPC=HPC, hb=hb, wqkv=wqkv, consts=consts, oT=oT)
        pg.emit(toks)
    return nc


def hg_consts():
    c = np.zeros((P, P + 64), np.float32)
    c[:, :P] = np.eye(P)
    s = np.arange(P)[:, None]
    t = np.arange(64)[None, :]
    c[:, P:] = ((s <= t) & (s < 64)).astype(np.float32)
    return c


def moba_consts(NBLK):
    c = np.zeros((P, 2 * P + NBLK * NBLK), np.float32)
    c[:, :P] = np.eye(P)
    q = np.arange(P)[:, None]
    k = np.arange(P)[None, :]
    c[:, P:2 * P] = np.where(k <= q, 0.0, NEG)
    j = np.arange(NBLK)[:, None]
    n = np.arange(NBLK)[None, :]
    c[:, 2 * P:] = np.where(n < j, 0.0, -1e30).reshape(1, -1)
    return c


def _run(nc, in_maps):
    res = run_bass_kernel_spmd(nc, in_maps, core_ids=list(range(NCORES)))
    return res.results


def _head_cols(w, nsec, c):
    D = D_MODEL
    return np.ascontiguousarray(np.stack([w[:, s * D + c * HPC * P: s * D + (c + 1) * HPC * P] for s in range(nsec)], 0))


def _tok_slices(full_T):
    return [np.ascontiguousarray(full_T[:, c * TPC:(c + 1) * TPC]) for c in range(NCORES)]


def kernel(**inp):
    f32 = lambda a: np.asarray(a, np.float32)
    x = f32(inp["x"])[0]
    xT = np.ascontiguousarray(x.T)
    nc1 = build_token_launch([dict(kind="ffn", name="f", want_b=True)])
    w_in, w_out = f32(inp["l0_ffn1_in"]), f32(inp["l0_ffn1_out"])
    g, b = _ln_lay(inp["l0_ln1_g"]), _ln_lay(inp["l0_ln1_b"])
    xs = _tok_slices(xT)
    r1 = _run(nc1, [{"xT": xs[c], "f_win": w_in, "f_wout": w_out, "f_g": g, "f_b": b} for c in range(NCORES)])
    del w_in, w_out
    y1 = [r["yT"] for r in r1]
    hb = np.ascontiguousarray(np.concatenate([r["ybT"] for r in r1], axis=1))
    nc2 = build_hgrn_launch()
    whg = f32(inp["l0_hg_in"])
    lbl = f32(inp["lb_logits"])
    ng = f32(inp["l0_hg_norm_g"])
    hc = hg_consts()
    maps = []
    for c in range(NCORES):
        hgp = np.zeros((P, HPC, 4), np.float32)
        for h in range(HPC):
            ch = (c * HPC + h) * P
            hgp[:, h, 0:3] = lbl[:, ch:ch + P].T
            hgp[:, h, 3] = ng[ch:ch + P]
        maps.append({"hb": hb, "w4": _head_cols(whg, 4, c), "hgp": hgp.reshape(P, -1), "consts": hc})
    r2 = _run(nc2, maps)
    del whg, maps
    oT = np.concatenate([r["oT"] for r in r2], axis=0)
    nc3 = build_token_launch([dict(kind="proj", name="p"), dict(kind="ffn", name="f"), dict(kind="ffn", name="h", want_b=True)])
    os_ = _tok_slices(oT)
    base = {"p_wout": f32(inp["l0_hg_out"]), "p_g": _ln_lay(inp["l0_ln2_g"]), "p_b": _ln_lay(inp["l0_ln2_b"]),
            "f_win": f32(inp["l0_ffn2_in"]), "f_wout": f32(inp["l0_ffn2_out"]), "f_g": _ln_lay(inp["l0_ln3_g"]), "f_b": _ln_lay(inp["l0_ln3_b"]),
            "h_win": f32(inp["l1_ffn1_in"]), "h_wout": f32(inp["l1_ffn1_out"]), "h_g": _ln_lay(inp["l1_ln1_g"]), "h_b": _ln_lay(inp["l1_ln1_b"])}
    r3 = _run(nc3, [dict(base, xT=y1[c], oin=os_[c]) for c in range(NCORES)])
    del base
    y4 = [r["yT"] for r in r3]
    hb = np.ascontiguousarray(np.concatenate([r["ybT"] for r in r3], axis=1))
    nc4 = build_moba_launch()
    wmb = f32(inp["l1_mb_in"])
    mc = moba_consts(SEQ // 256)
    r4 = _run(nc4, [{"hb": hb, "wqkv": _head_cols(wmb, 3, c), "consts": mc} for c in range(NCORES)])
    del wmb
    oT = np.concatenate([r["oT"] for r in r4], axis=0)
    nc5 = build_token_launch([dict(kind="proj", name="p"), dict(kind="ffn", name="f")])
    os_ = _tok_slices(oT)
    base = {"p_wout": f32(inp["l1_mb_out"]), "p_g": _ln_lay(inp["l1_ln2_g"]), "p_b": _ln_lay(inp["l1_ln2_b"]),
            "f_win": f32(inp["l1_ffn2_in"]), "f_wout": f32(inp["l1_ffn2_out"]), "f_g": _ln_lay(inp["l1_ln3_g"]), "f_b": _ln_lay(inp["l1_ln3_b"])}
    r5 = _run(nc5, [dict(base, xT=y4[c], oin=os_[c]) for c in range(NCORES)])
    yT = np.concatenate([r["yT"] for r in r5], axis=1)
    return np.ascontiguousarray(yT.T)[None].astype(np.float32)
```
